# Optimizing a Trainium2 kernel written in Bass

```python
import math
import jax, jax.numpy as jnp
from jax import lax
import numpy as np

D_MODEL = 1024
BATCH = 8
SEQ = 8192
DEPTH = 4

HEAD_DIM = 64
NSA_HEADS = 12
NSA_KV_HEADS = 4
NSA_GROUP = NSA_HEADS // NSA_KV_HEADS
NSA_WIDTH = NSA_HEADS * HEAD_DIM
NSA_KV_WIDTH = NSA_KV_HEADS * HEAD_DIM
N_BRANCH = 3
CMP_LEN = 32
CMP_STRIDE = 16
CMP_HIDDEN = 4 * HEAD_DIM
SEL_BLOCK = 64
SEL_TOP = 16
WINDOW = 512
Q_BLOCK = 128
GMLP_CHUNK = 128
GMLP_GROUPS = 12
GMLP_WIDTH = 3 * D_MODEL // 4
GMLP_GROUP_DIM = GMLP_WIDTH // GMLP_GROUPS
MEM_LEN = 256
MEM_HEADS = 4
MEM_WIDTH = MEM_HEADS * HEAD_DIM
D_FF = -(-8 * D_MODEL // (3 * 256)) * 256
ROPE_THETA = 10000.0
NORM_EPS = 1e-6
NEG_INF = -1e30
FORCE_BONUS = 1e4
NSA_IN = NSA_WIDTH + 6 * NSA_KV_WIDTH + N_BRANCH * NSA_HEADS + MEM_WIDTH
GMLP_IN = 2 * GMLP_WIDTH + MEM_WIDTH

kernel_name = 'hybrid_gmlp_nsa_interleaved'


def rmsnorm(x, g):
    xf = x.astype(jnp.float32)
    y = xf * lax.rsqrt(jnp.mean(xf * xf, axis=-1, keepdims=True) + NORM_EPS)
    return y.astype(x.dtype) * g


def rope(x):
    s, d = x.shape[1], x.shape[-1]
    half = d // 2
    pos = jnp.arange(s, dtype=jnp.float32)
    inv = ROPE_THETA ** (-jnp.arange(half, dtype=jnp.float32) / half)
    ang = pos[:, None] * inv[None, :]
    cos = jnp.cos(ang)[None, :, None, :]
    sin = jnp.sin(ang)[None, :, None, :]
    xf = x.astype(jnp.float32)
    x1, x2 = xf[..., :half], xf[..., half:]
    return jnp.concatenate([x1 * cos - x2 * sin, x2 * cos + x1 * sin], axis=-1).astype(x.dtype)


def swiglu(h, wg, wu, wd):
    return (jax.nn.silu(h @ wg) * (h @ wu)) @ wd


def mem_cross_attention(q, mem_n, w_kv):
    b, s, _ = q.shape
    m = mem_n.shape[1]
    q = q.reshape(b, s, MEM_HEADS, HEAD_DIM)
    k, v = jnp.split(mem_n @ w_kv, 2, axis=-1)
    k = k.reshape(b, m, MEM_HEADS, HEAD_DIM)
    v = v.reshape(b, m, MEM_HEADS, HEAD_DIM)
    sc = jnp.einsum('bshd,bmhd->bhsm', q, k).astype(jnp.float32) * (HEAD_DIM ** -0.5)
    p = jax.nn.softmax(sc, axis=-1).astype(v.dtype)
    return jnp.einsum('bhsm,bmhd->bshd', p, v).reshape(b, s, MEM_WIDTH)


def gmlp_mixer(h, w_in, v_gain, w_s, b_s):
    b, s, _ = h.shape
    proj = h @ w_in
    z, mem_q = proj[..., :2 * GMLP_WIDTH], proj[..., 2 * GMLP_WIDTH:]
    u, v = jnp.split(jax.nn.gelu(z), 2, axis=-1)
    nc = s // GMLP_CHUNK
    vn = rmsnorm(v, v_gain).reshape(b, nc, GMLP_CHUNK, GMLP_GROUPS, GMLP_GROUP_DIM)
    causal = jnp.tril(jnp.ones((GMLP_CHUNK, GMLP_CHUNK), dtype=bool))
    w = jnp.where(causal[None], w_s, jnp.zeros((), w_s.dtype))
    mixed = jnp.einsum('gts,bcsgd->bctgd', w, vn) + b_s.T[None, None, :, :, None]
    return u * mixed.reshape(b, s, GMLP_WIDTH), mem_q


def compress_blocks(k, pe, w1, w2):
    b, s, g, d = k.shape
    kc = k.reshape(b, s // CMP_STRIDE, CMP_STRIDE, g, d)
    blocks = jnp.concatenate([kc[:, :-1], kc[:, 1:]], axis=2) + pe[None, None, :, None, :]
    flat = blocks.transpose(0, 1, 3, 2, 4).reshape(b, s // CMP_STRIDE - 1, g, CMP_LEN * d)
    return jax.nn.silu(flat @ w1) @ w2


def selection_map(s):
    nc = s // CMP_STRIDE - 1
    nb = s // SEL_BLOCK
    c0 = jnp.arange(nc) * CMP_STRIDE
    b0 = jnp.arange(nb) * SEL_BLOCK
    lo = jnp.maximum(c0[:, None], b0[None, :])
    hi = jnp.minimum(c0[:, None] + CMP_LEN, b0[None, :] + SEL_BLOCK)
    return jnp.clip(hi - lo, 0).astype(jnp.float32) / CMP_LEN


def nsa_single(q, kc, vc, ks, vs, kw, vw, gates, sel_map):
    s = q.shape[0]
    nc = kc.shape[0]
    nb = s // SEL_BLOCK
    n_sel = min(SEL_TOP, nb)
    scale = HEAD_DIM ** -0.5
    qg = q.reshape(s, NSA_KV_HEADS, NSA_GROUP, HEAD_DIM)
    gg = gates.reshape(s, NSA_KV_HEADS, NSA_GROUP, N_BRANCH)
    ks_blk = ks.reshape(nb, SEL_BLOCK, NSA_KV_HEADS, HEAD_DIM).transpose(2, 0, 1, 3)
    vs_blk = vs.reshape(nb, SEL_BLOCK, NSA_KV_HEADS, HEAD_DIM).transpose(2, 0, 1, 3)
    kw_pad = jnp.pad(kw, ((WINDOW, 0), (0, 0), (0, 0)))
    vw_pad = jnp.pad(vw, ((WINDOW, 0), (0, 0), (0, 0)))
    cmp_end = jnp.arange(nc) * CMP_STRIDE + (CMP_LEN - 1)
    blk_ids = jnp.arange(nb)
    in_blk = jnp.arange(SEL_BLOCK)
    win_off = jnp.arange(Q_BLOCK + WINDOW) - WINDOW
    gather = jax.vmap(lambda kb, ib: kb[ib], in_axes=(0, 1), out_axes=1)

    def block(i):
        t0 = i * Q_BLOCK
        tq = t0 + jnp.arange(Q_BLOCK)
        qb = lax.dynamic_slice_in_dim(qg, t0, Q_BLOCK, 0)
        gb = lax.dynamic_slice_in_dim(gg, t0, Q_BLOCK, 0)
        ok_c = cmp_end[None, :] <= tq[:, None]
        sc = jnp.einsum('qgrd,cgd->qgrc', qb, kc).astype(jnp.float32) * scale
        sc = jnp.where(ok_c[:, None, None, :], sc, NEG_INF)
        p_c = jax.nn.softmax(sc, axis=-1) * jnp.any(ok_c, axis=-1).astype(jnp.float32)[:, None, None, None]
        o_c = jnp.einsum('qgrc,cgd->qgrd', p_c.astype(vc.dtype), vc)
        imp = jnp.einsum('qgrc,cn->qgn', p_c, sel_map)
        cur = tq // SEL_BLOCK
        forced = (blk_ids[None, :] == 0) | (blk_ids[None, :] == cur[:, None]) | (blk_ids[None, :] == cur[:, None] - 1)
        imp = imp + jnp.where(forced, FORCE_BONUS, 0.0)[:, None, :]
        imp = jnp.where((blk_ids[None, :] <= cur[:, None])[:, None, :], imp, NEG_INF)
        _, idx = lax.top_k(imp, n_sel)
        k_sel = gather(ks_blk, idx).reshape(Q_BLOCK, NSA_KV_HEADS, n_sel * SEL_BLOCK, HEAD_DIM)
        v_sel = gather(vs_blk, idx).reshape(Q_BLOCK, NSA_KV_HEADS, n_sel * SEL_BLOCK, HEAD_DIM)
        pos_sel = (idx[..., None] * SEL_BLOCK + in_blk).reshape(Q_BLOCK, NSA_KV_HEADS, n_sel * SEL_BLOCK)
        ok_s = pos_sel <= tq[:, None, None]
        ss = jnp.einsum('qgrd,qgmd->qgrm', qb, k_sel).astype(jnp.float32) * scale
        ss = jnp.where(ok_s[:, :, None, :], ss, NEG_INF)
        o_s = jnp.einsum('qgrm,qgmd->qgrd', jax.nn.softmax(ss, axis=-1).astype(v_sel.dtype), v_sel)
        k_win = lax.dynamic_slice_in_dim(kw_pad, t0, Q_BLOCK + WINDOW, 0)
        v_win = lax.dynamic_slice_in_dim(vw_pad, t0, Q_BLOCK + WINDOW, 0)
        pos_w = t0 + win_off
        ok_w = (pos_w[None, :] >= 0) & (pos_w[None, :] <= tq[:, None]) & (pos_w[None, :] > tq[:, None] - WINDOW)
        sw = jnp.einsum('qgrd,kgd->qgrk', qb, k_win).astype(jnp.float32) * scale
        sw = jnp.where(ok_w[:, None, None, :], sw, NEG_INF)
        o_w = jnp.einsum('qgrk,kgd->qgrd', jax.nn.softmax(sw, axis=-1).astype(v_win.dtype), v_win)
        o = gb[..., 0:1] * o_c + gb[..., 1:2] * o_s + gb[..., 2:3] * o_w
        return o.reshape(Q_BLOCK, NSA_WIDTH)

    out = lax.map(block, jnp.arange(s // Q_BLOCK))
    return out.reshape(s, NSA_WIDTH)


def nsa_mixer(h, w_in, pe_k, pe_v, ck_w1, ck_w2, cv_w1, cv_w2):
    b, s, _ = h.shape
    sizes = [NSA_WIDTH] + [NSA_KV_WIDTH] * 6 + [N_BRANCH * NSA_HEADS, MEM_WIDTH]
    offsets = [int(o) for o in np.cumsum(sizes)[:-1]]
    q, k_cmp, v_cmp, k_slc, v_slc, k_win, v_win, g, mem_q = jnp.split(h @ w_in, offsets, axis=-1)
    heads = lambda t, n: t.reshape(b, s, n, HEAD_DIM)
    q = rope(heads(q, NSA_HEADS))
    kc = compress_blocks(rope(heads(k_cmp, NSA_KV_HEADS)), pe_k, ck_w1, ck_w2)
    vc = compress_blocks(heads(v_cmp, NSA_KV_HEADS), pe_v, cv_w1, cv_w2)
    ks = rope(heads(k_slc, NSA_KV_HEADS))
    vs = heads(v_slc, NSA_KV_HEADS)
    kw = rope(heads(k_win, NSA_KV_HEADS))
    vw = heads(v_win, NSA_KV_HEADS)
    gates = jax.nn.sigmoid(g.astype(jnp.float32)).astype(h.dtype).reshape(b, s, NSA_HEADS, N_BRANCH)
    sel_map = selection_map(s)
    out = lax.map(lambda a: nsa_single(a[0], a[1], a[2], a[3], a[4], a[5], a[6], a[7], sel_map),
                  (q, kc, vc, ks, vs, kw, vw, gates))
    return out, mem_q


def setup_inputs(seed: int = 0) -> dict:
    key = jax.random.key(seed)
    k = jax.random.split(key, 23)
    n_a = (DEPTH + 1) // 2
    n_b = DEPTH // 2
    nrm = lambda kk, shape, sc: jax.random.normal(kk, shape, jnp.float32) * sc
    out_sc = (2.0 * DEPTH) ** -0.5
    return {
        'x': nrm(k[0], (BATCH, SEQ, D_MODEL), 1.0),
        'mem': nrm(k[1], (BATCH, MEM_LEN, D_MODEL), 1.0),
        'norm_mix': 1.0 + nrm(k[2], (DEPTH, D_MODEL), 0.02),
        'norm_ffn': 1.0 + nrm(k[3], (DEPTH, D_MODEL), 0.02),
        'norm_mem': 1.0 + nrm(k[4], (DEPTH, D_MODEL), 0.02),
        'norm_final': 1.0 + nrm(k[5], (D_MODEL,), 0.02),
        'w_mem_kv': nrm(k[6], (DEPTH, D_MODEL, 2 * MEM_WIDTH), D_MODEL ** -0.5),
        'ffn_w_gate': nrm(k[7], (DEPTH, D_MODEL, D_FF), D_MODEL ** -0.5),
        'ffn_w_up': nrm(k[8], (DEPTH, D_MODEL, D_FF), D_MODEL ** -0.5),
        'ffn_w_down': nrm(k[9], (DEPTH, D_FF, D_MODEL), D_FF ** -0.5 * out_sc),
        'gmlp_w_in': nrm(k[10], (n_a, D_MODEL, GMLP_IN), D_MODEL ** -0.5),
        'gmlp_v_norm': 1.0 + nrm(k[11], (n_a, GMLP_WIDTH), 0.02),
        'gmlp_w_s': nrm(k[12], (n_a, GMLP_GROUPS, GMLP_CHUNK, GMLP_CHUNK), GMLP_CHUNK ** -0.5),
        'gmlp_b_s': 1.0 + nrm(k[13], (n_a, GMLP_GROUPS, GMLP_CHUNK), 0.02),
        'gmlp_w_out': nrm(k[14], (n_a, GMLP_WIDTH + MEM_WIDTH, D_MODEL), (GMLP_WIDTH + MEM_WIDTH) ** -0.5 * out_sc),
        'nsa_w_in': nrm(k[15], (n_b, D_MODEL, NSA_IN), D_MODEL ** -0.5),
        'nsa_pe_k': nrm(k[16], (n_b, CMP_LEN, HEAD_DIM), 0.1),
        'nsa_pe_v': nrm(k[17], (n_b, CMP_LEN, HEAD_DIM), 0.1),
        'nsa_ck_w1': nrm(k[18], (n_b, CMP_LEN * HEAD_DIM, CMP_HIDDEN), (CMP_LEN * HEAD_DIM) ** -0.5),
        'nsa_ck_w2': nrm(k[19], (n_b, CMP_HIDDEN, HEAD_DIM), CMP_HIDDEN ** -0.5),
        'nsa_cv_w1': nrm(k[20], (n_b, CMP_LEN * HEAD_DIM, CMP_HIDDEN), (CMP_LEN * HEAD_DIM) ** -0.5),
        'nsa_cv_w2': nrm(k[21], (n_b, CMP_HIDDEN, HEAD_DIM), CMP_HIDDEN ** -0.5),
        'nsa_w_out': nrm(k[22], (n_b, NSA_WIDTH + MEM_WIDTH, D_MODEL), (NSA_WIDTH + MEM_WIDTH) ** -0.5 * out_sc),
    }


def reference(x, mem, norm_mix, norm_ffn, norm_mem, norm_final, w_mem_kv, ffn_w_gate, ffn_w_up, ffn_w_down,
              gmlp_w_in, gmlp_v_norm, gmlp_w_s, gmlp_b_s, gmlp_w_out,
              nsa_w_in, nsa_pe_k, nsa_pe_v, nsa_ck_w1, nsa_ck_w2, nsa_cv_w1, nsa_cv_w2, nsa_w_out):
    for i in range(DEPTH):
        j = i // 2
        h = rmsnorm(x, norm_mix[i])
        mem_n = rmsnorm(mem, norm_mem[i])
        if i % 2 == 0:
            mix, mem_q = gmlp_mixer(h, gmlp_w_in[j], gmlp_v_norm[j], gmlp_w_s[j], gmlp_b_s[j])
            w_out = gmlp_w_out[j]
        else:
            mix, mem_q = nsa_mixer(h, nsa_w_in[j], nsa_pe_k[j], nsa_pe_v[j], nsa_ck_w1[j], nsa_ck_w2[j],
                                   nsa_cv_w1[j], nsa_cv_w2[j])
            w_out = nsa_w_out[j]
        mem_o = mem_cross_attention(mem_q, mem_n, w_mem_kv[i])
        x = x + jnp.concatenate([mix, mem_o], axis=-1) @ w_out
        h = rmsnorm(x, norm_ffn[i])
        x = x + swiglu(h, ffn_w_gate[i], ffn_w_up[i], ffn_w_down[i])
    return rmsnorm(x, norm_final)
```

```python
import math
from contextlib import ExitStack

import numpy as np
import concourse.bass as bass
import concourse.mybir as mybir
from concourse.bass_utils import run_bass_kernel_spmd

F32 = mybir.dt.float32
BF16 = mybir.dt.bfloat16
AF = mybir.ActivationFunctionType
ALU = mybir.AluOpType
AX = mybir.AxisListType

D = 1024
DFF = 2816
NFC = DFF // 128
EPS = 1e-6
N_CORES = 8


class Buf:
    __slots__ = ("t", "name", "writers", "readers", "guard", "dsem", "children", "parent")

    def __init__(self, t, name, parent=None):
        self.t = t
        self.name = name
        self.writers = {}
        self.readers = {}
        self.guard = {}
        self.dsem = None
        self.children = {}
        self.parent = parent

    def __getitem__(self, idx):
        return self.t[idx]

    def sub(self, key):
        c = self.children.get(key)
        if c is None:
            c = Buf(self.t, "%s.%s" % (self.name, key), parent=self)
            c.writers = dict(self.writers)
            c.readers = dict(self.readers)
            c.guard = dict(self.guard)
            self.children[key] = c
        return c

    def nodes(self):
        if self.children:
            return [self] + list(self.children.values())
        return [self]

    def sem_owner(self):
        return self.parent if self.parent is not None else self


def _merge(dst, src):
    for k, v in src.items():
        if dst.get(k, 0) < v:
            dst[k] = v


class Sched:
    CE = ("pe", "dve", "act", "pool")

    def __init__(self, nc, es, n_dma_sems=40):
        self.nc = nc
        self.engs = {"pe": nc.tensor, "dve": nc.vector, "act": nc.scalar, "pool": nc.gpsimd, "sp": nc.sync}
        self.sems = {}
        self.cnt = {}
        for k in self.CE:
            self.sems[k] = es.enter_context(nc.semaphore("sem_" + k))
            self.cnt[k] = 0
        self.free_dsems = []
        for i in range(n_dma_sems):
            key = "d%d" % i
            self.sems[key] = es.enter_context(nc.semaphore("sem_" + key))
            self.cnt[key] = 0
            self.free_dsems.append(key)
        self.waited = {k: {} for k in self.engs}
        self.phase_bufs = []
        self.pes = None
        self.n_ins = 0
        self.n_wait = 0
        self._grp = None

    def begin_phase(self):
        self.pes = ExitStack()
        self.pes.__enter__()
        self.phase_bufs = []
        self.phase_id = getattr(self, "phase_id", 0) + 1

    def end_phase(self):
        self.barrier()
        for b in self.phase_bufs:
            if b.dsem is not None:
                self.free_dsems.append(b.dsem)
                b.dsem = None
        self.phase_bufs = []
        self.pes.__exit__(None, None, None)
        self.pes = None

    def sbuf(self, name, shape, dtype):
        t = self.pes.enter_context(self.nc.sbuf_tensor("p%d_%s" % (self.phase_id, name), list(shape), dtype))
        b = Buf(t, name)
        self.phase_bufs.append(b)
        return b

    def psum(self, name, shape, dtype):
        t = self.pes.enter_context(self.nc.psum_tensor("p%d_%s" % (self.phase_id, name), list(shape), dtype))
        b = Buf(t, name)
        self.phase_bufs.append(b)
        return b

    def _emit_waits(self, eng, deps):
        w = self.waited[eng]
        e = self.engs[eng]
        for k, v in deps.items():
            if k == "pe" and eng == "pe":
                continue
            if w.get(k, 0) >= v:
                continue
            e.wait_ge(self.sems[k], v)
            w[k] = v
            self.n_wait += 1

    def _deps(self, reads, writes):
        deps = {}
        for bb in reads:
            for b in bb.nodes():
                _merge(deps, b.writers)
        for bb in writes:
            for b in bb.nodes():
                if b.readers:
                    g = dict(b.readers)
                    _merge(g, b.writers)
                    b.guard = g
                    b.writers = {}
                    b.readers = {}
                _merge(deps, b.guard)
                _merge(deps, b.writers)
        return deps

    def _record(self, reads, writes, key, val):
        for bb in reads:
            for b in bb.nodes():
                if b.readers.get(key, 0) < val:
                    b.readers[key] = val
        for bb in writes:
            for b in bb.nodes():
                if b.writers.get(key, 0) < val:
                    b.writers[key] = val

    def op(self, eng, fn, reads=(), writes=(), signal=True):
        deps = self._deps(reads, writes)
        self._emit_waits(eng, deps)
        ins = fn()
        self.n_ins += 1
        if signal:
            self.cnt[eng] += 1
            ins.then_inc(self.sems[eng], 1)
            val = self.cnt[eng]
        else:
            val = self.cnt[eng] + 1
        self._record(reads, writes, eng, val)
        return ins

    def dma(self, q, out, in_, reads=(), writes=(), sem_buf=None, **kw):
        if sem_buf is None:
            sem_buf = writes[0] if writes else reads[0]
        sem_buf = sem_buf.sem_owner()
        if sem_buf.dsem is None:
            sem_buf.dsem = self.free_dsems.pop()
        key = sem_buf.dsem
        deps = self._deps(reads, writes)
        self._emit_waits(q, deps)
        ins = self.engs[q].dma_start(out=out, in_=in_, **kw)
        self.cnt[key] += 16
        ins.then_inc(self.sems[key], 16)
        self.n_ins += 1
        if self._grp is not None:
            self._grp.append((reads, writes, key))
        else:
            self._record(reads, writes, key, self.cnt[key])
        return ins

    def dma_group_begin(self):
        self._grp = []

    def dma_group_end(self):
        for reads, writes, key in self._grp:
            self._record(reads, writes, key, self.cnt[key])
        self._grp = None

    def barrier(self):
        allv = {k: v for k, v in self.cnt.items() if v > 0}
        for eng in self.engs:
            deps = {k: v for k, v in allv.items() if k != eng or eng == "sp"}
            if eng in self.CE and self.cnt[eng] > 0:
                deps[eng] = self.cnt[eng]
            w = self.waited[eng]
            e = self.engs[eng]
            for k, v in deps.items():
                if w.get(k, 0) >= v:
                    continue
                e.wait_ge(self.sems[k], v)
                w[k] = v


def make_identity(S, name="ident", dtype=BF16):
    nc = S.nc
    ones = S.sbuf(name + "_ones", [128, 128], dtype)
    ident = S.sbuf(name, [128, 128], dtype)
    S.op("pool", lambda: nc.gpsimd.memset(ones[:], 1.0), writes=[ones])
    S.op("pool", lambda: nc.gpsimd.affine_select(
        out=ident[:], in_=ones[:], pattern=[[1, 128]], compare_op=ALU.is_equal, fill=0.0,
        base=0, channel_multiplier=-1), reads=[ones], writes=[ident])
    return ident


_cast_rr = [0]


def cast_copy(S, out_ap, in_ap, reads, writes, engines=("dve", "pool", "act")):
    nc = S.nc
    e = engines[_cast_rr[0] % len(engines)]
    _cast_rr[0] += 1
    if e == "dve":
        S.op("dve", lambda: nc.vector.tensor_copy(out=out_ap, in_=in_ap), reads=reads, writes=writes)
    elif e == "pool":
        S.op("pool", lambda: nc.gpsimd.tensor_copy(out=out_ap, in_=in_ap), reads=reads, writes=writes)
    else:
        S.op("act", lambda: nc.scalar.copy(out=out_ap, in_=in_ap), reads=reads, writes=writes)


def load_gain_bc(S, name, g_ap_1d, n):
    t = S.sbuf(name, [128, n], F32)
    S.dma("sp", t[:], g_ap_1d.partition_broadcast(128), writes=[t])
    return t


def rms_rstd(S, ssq, rstd, n_feat, ncols, c0=0):
    nc = S.nc
    if getattr(S, "_mhalf_phase", None) != S.phase_id:
        S._mhalf = S.sbuf("mhalf", [128, 16], F32)
        S._mhalf_phase = S.phase_id
        S.op("pool", lambda: nc.gpsimd.memset(S._mhalf[:], -0.5), writes=[S._mhalf])
    mh = S._mhalf
    S.op("dve", lambda: nc.vector.tensor_scalar(
        out=rstd[:, c0:ncols], in0=ssq[:, c0:ncols], scalar1=1.0 / n_feat, scalar2=EPS,
        op0=ALU.mult, op1=ALU.add), reads=[ssq], writes=[rstd])
    S.op("pool", lambda: nc.gpsimd.tensor_tensor(
        out=rstd[:, c0:ncols], in0=rstd[:, c0:ncols], in1=mh[:, c0:ncols], op=ALU.pow),
        reads=[rstd, mh], writes=[rstd])


def ffn_phase(S, x_dram, n_tok, wg, wu, wd, gain, T=256, final_gain=None):
    nc = S.nc
    S.begin_phase()
    nsub = T // 128
    ntile = n_tok // T
    ident = make_identity(S)
    g_bc = load_gain_bc(S, "g_bc", gain, D)
    gf_bc = load_gain_bc(S, "gf_bc", final_gain, D) if final_gain is not None else None
    fssq = S.sbuf("fssq", [128, 8], F32)
    frstd = S.sbuf("frstd", [128, 8], F32)

    wg_sb = S.sbuf("wg_sb", [128, 8, DFF], BF16)
    wu_sb = S.sbuf("wu_sb", [128, 8, DFF], BF16)
    wd_sb = S.sbuf("wd_sb", [128, NFC, D], BF16)
    stage = [S.sbuf("stage%d" % i, [128, DFF], F32) for i in range(2)]
    si = 0
    for w_dram, w_sb in ((wg, wg_sb), (wu, wu_sb)):
        for k in range(8):
            st = stage[si % 2]
            si += 1
            S.dma("sp", st[:], w_dram[k * 128:(k + 1) * 128, :], writes=[st])
            cast_copy(S, w_sb[:, k, :], st[:], [st], [w_sb])
    wd_v = wd.rearrange("(c p) n -> p c n", p=128)
    for c2 in range(NFC // 2):
        st = stage[si % 2]
        si += 1
        S.dma("sp", st[:, 0:2 * D].rearrange("p (c n) -> p c n", c=2), wd_v[:, 2 * c2:2 * c2 + 2, :], writes=[st])
        cast_copy(S, wd_sb[:, 2 * c2:2 * c2 + 2, :], st[:, 0:2 * D].rearrange("p (c n) -> p c n", c=2), [st], [wd_sb])

    xt = [S.sbuf("xt%d" % i, [128, nsub, D], F32) for i in range(2)]
    hT = [S.sbuf("hT%d" % i, [128, 8, T], BF16) for i in range(2)]
    h = S.sbuf("h", [128, nsub, D], BF16)
    junk = S.sbuf("junk", [128, D], BF16)
    ssq = S.sbuf("ssq", [128, 8], F32)
    rstd = S.sbuf("rstd", [128, 8], F32)
    aT = S.sbuf("aT", [128, NFC, T], BF16)
    sg = [S.sbuf("sg%d" % i, [128, T], F32) for i in range(2)]
    tp = [S.psum("tp%d" % i, [128, 8, 128], BF16) for i in range(2)]
    pg = [S.psum("pg%d" % i, [128, 512], F32) for i in range(2)]
    pu = [S.psum("pu%d" % i, [128, 512], F32) for i in range(2)]
    po = [S.psum("po%d" % i, [128, 512], F32) for i in range(2)]
    x_v = x_dram.rearrange("(t s p) d -> t p s d", p=128, s=nsub)
    cnt = {"tp": 0, "g": 0, "o": 0}

    def stage_a(i):
        X = xt[i % 2]
        HT = hT[i % 2]
        S.dma("sp", X[:], x_v[i], writes=[X])
        for s in range(nsub):
            S.op("act", lambda s=s: nc.scalar.activation(
                out=junk[:], in_=X[:, s, :], func=AF.Square, accum_out=ssq[:, s:s + 1]),
                reads=[X], writes=[junk, ssq])
        rms_rstd(S, ssq, rstd, D, nsub)
        for s in range(nsub):
            S.op("dve", lambda s=s: nc.vector.scalar_tensor_tensor(
                out=h[:, s, :], in0=X[:, s, :], scalar=rstd[:, s:s + 1], in1=g_bc[:],
                op0=ALU.mult, op1=ALU.mult), reads=[X, rstd, g_bc], writes=[h])
        for s in range(nsub):
            P = tp[cnt["tp"] % 2]
            cnt["tp"] += 1
            for k in range(8):
                S.op("pe", lambda k=k, s=s, P=P: nc.tensor.transpose(
                    out=P[:, k, :], in_=h[:, s, k * 128:(k + 1) * 128], identity=ident[:]),
                    reads=[h, ident], writes=[P], signal=(k == 7))
            S.op("act", lambda s=s, P=P: nc.scalar.copy(out=HT[:, :, s * 128:(s + 1) * 128], in_=P[:]),
                 reads=[P], writes=[HT])

    def stage_b(i):
        HT = hT[i % 2]
        for fc in range(NFC):
            G = pg[cnt["g"] % 2]
            U = pu[cnt["g"] % 2]
            SG = sg[cnt["g"] % 2]
            cnt["g"] += 1
            for k in range(8):
                S.op("pe", lambda k=k, G=G: nc.tensor.matmul(
                    out=G[:, 0:T], lhsT=wg_sb[:, k, fc * 128:(fc + 1) * 128], rhs=HT[:, k, :],
                    start=(k == 0), stop=(k == 7)), reads=[wg_sb, HT], writes=[G], signal=(k == 7))
            for k in range(8):
                S.op("pe", lambda k=k, U=U: nc.tensor.matmul(
                    out=U[:, 0:T], lhsT=wu_sb[:, k, fc * 128:(fc + 1) * 128], rhs=HT[:, k, :],
                    start=(k == 0), stop=(k == 7)), reads=[wu_sb, HT], writes=[U], signal=(k == 7))
            S.op("act", lambda G=G, SG=SG: nc.scalar.activation(out=SG[:], in_=G[:, 0:T], func=AF.Silu),
                 reads=[G], writes=[SG])
            S.op("dve", lambda U=U, SG=SG: nc.vector.tensor_tensor(
                out=aT[:, fc, :], in0=SG[:], in1=U[:, 0:T], op=ALU.mult), reads=[SG, U], writes=[aT])

    def stage_c(i):
        X = xt[i % 2]
        for s in range(nsub):
            for hf in range(2):
                O = po[cnt["o"] % 2]
                cnt["o"] += 1
                for fc in range(NFC):
                    S.op("pe", lambda fc=fc, O=O: nc.tensor.matmul(
                        out=O[:], lhsT=aT[:, fc, s * 128:(s + 1) * 128], rhs=wd_sb[:, fc, hf * 512:(hf + 1) * 512],
                        start=(fc == 0), stop=(fc == NFC - 1)), reads=[aT, wd_sb], writes=[O],
                        signal=(fc == NFC - 1))
                S.op("dve", lambda O=O: nc.vector.tensor_tensor(
                    out=X[:, s, hf * 512:(hf + 1) * 512], in0=X[:, s, hf * 512:(hf + 1) * 512], in1=O[:],
                    op=ALU.add), reads=[X, O], writes=[X])
        if gf_bc is not None:
            for s in range(nsub):
                S.op("act", lambda s=s: nc.scalar.activation(
                    out=junk[:], in_=X[:, s, :], func=AF.Square, accum_out=fssq[:, s:s + 1]),
                    reads=[X], writes=[junk, fssq])
            rms_rstd(S, fssq, frstd, D, nsub)
            for s in range(nsub):
                S.op("dve", lambda s=s: nc.vector.scalar_tensor_tensor(
                    out=X[:, s, :], in0=X[:, s, :], scalar=frstd[:, s:s + 1], in1=gf_bc[:],
                    op0=ALU.mult, op1=ALU.mult), reads=[X, frstd, gf_bc], writes=[X])
        S.dma("sp", x_v[i], X[:], reads=[X])

    stage_a(0)
    for i in range(ntile):
        stage_b(i)
        if i + 1 < ntile:
            stage_a(i + 1)
        stage_c(i)
    S.end_phase()


def final_norm_phase(S, x_dram, out_dram, n_tok, gain, T=512):
    nc = S.nc
    S.begin_phase()
    nsub = T // 128
    ntile = n_tok // T
    g_bc = load_gain_bc(S, "g_bc", gain, D)
    xt = [S.sbuf("xt%d" % i, [128, nsub, D], F32) for i in range(2)]
    junk = S.sbuf("junk", [128, D], BF16)
    ssq = S.sbuf("ssq", [128, 8], F32)
    rstd = S.sbuf("rstd", [128, 8], F32)
    x_v = x_dram.rearrange("(t s p) d -> t p s d", p=128, s=nsub)
    o_v = out_dram.rearrange("(t s p) d -> t p s d", p=128, s=nsub)
    for i in range(ntile):
        X = xt[i % 2]
        S.dma("sp", X[:], x_v[i], writes=[X])
        for s in range(nsub):
            S.op("act", lambda s=s: nc.scalar.activation(
                out=junk[:], in_=X[:, s, :], func=AF.Square, accum_out=ssq[:, s:s + 1]),
                reads=[X], writes=[junk, ssq])
        rms_rstd(S, ssq, rstd, D, nsub)
        for s in range(nsub):
            S.op("dve", lambda s=s: nc.vector.scalar_tensor_tensor(
                out=X[:, s, :], in0=X[:, s, :], scalar=rstd[:, s:s + 1], in1=g_bc[:],
                op0=ALU.mult, op1=ALU.mult), reads=[X, rstd, g_bc], writes=[X])
        S.dma("sp", o_v[i], X[:], reads=[X])
    S.end_phase()


def copy_phase(S, src_dram, dst_dram, n_tok, T=512):
    S.begin_phase()
    nsub = T // 128
    xt = [S.sbuf("ct%d" % i, [128, nsub, D], F32) for i in range(2)]
    s_v = src_dram.rearrange("(t s p) d -> t p s d", p=128, s=nsub)
    d_v = dst_dram.rearrange("(t s p) d -> t p s d", p=128, s=nsub)
    for i in range(n_tok // T):
        X = xt[i % 2]
        S.dma("sp", X[:], s_v[i], writes=[X])
        S.dma("sp", d_v[i], X[:], reads=[X])
    S.end_phase()


def load_weight_bf16(S, w_sb, w_dram, stage, col_lo=0, col_hi=None, dst_lo=0, q="sp"):
    nk = w_dram.shape[0] // 128
    if col_hi is None:
        col_hi = w_dram.shape[1]
    n = col_hi - col_lo
    for k in range(nk):
        st = stage[S._stage_i % len(stage)]
        S._stage_i += 1
        S.dma(q, st[:, 0:n], w_dram[k * 128:(k + 1) * 128, col_lo:col_hi], writes=[st])
        cast_copy(S, w_sb[:, k, dst_lo:dst_lo + n], st[:, 0:n], [st], [w_sb])


def mem_kv_prep(S, banks, tpb, ident, mem_b, g_mem, w_kv, stage):
    nc = S.nc
    gm_bc = load_gain_bc(S, "gm_bc", g_mem, D)
    wkv_sb = S.sbuf("wkv_sb", [128, 8, 512], BF16)
    load_weight_bf16(S, wkv_sb, w_kv, stage)
    mx = S.sbuf("mem_x", [128, 2, D], F32)
    mh = S.sbuf("mem_h", [128, 2, D], BF16)
    mjunk = S.sbuf("mem_junk", [128, D], BF16)
    mssq = S.sbuf("mem_ssq", [128, 8], F32)
    mrstd = S.sbuf("mem_rstd", [128, 8], F32)
    memT = S.sbuf("memT", [128, 8, 256], BF16)
    kT = S.sbuf("kT_mem", [128, 2, 256], BF16)
    vm = S.sbuf("v_mem", [128, 2, 256], BF16)
    S.dma("sp", mx[:], mem_b.rearrange("(s p) d -> p s d", p=128), writes=[mx])
    for s in range(2):
        S.op("act", lambda s=s: nc.scalar.activation(
            out=mjunk[:], in_=mx[:, s, :], func=AF.Square, accum_out=mssq[:, s:s + 1]),
            reads=[mx], writes=[mjunk, mssq])
    rms_rstd(S, mssq, mrstd, D, 2)
    for s in range(2):
        S.op("dve", lambda s=s: nc.vector.scalar_tensor_tensor(
            out=mh[:, s, :], in0=mx[:, s, :], scalar=mrstd[:, s:s + 1], in1=gm_bc[:],
            op0=ALU.mult, op1=ALU.mult), reads=[mx, mrstd, gm_bc], writes=[mh])
    for s in range(2):
        for k in range(8):
            S.op("pe", lambda k=k, s=s: nc.tensor.transpose(
                out=tpb[:, k, :], in_=mh[:, s, k * 128:(k + 1) * 128], identity=ident[:]),
                reads=[mh, ident], writes=[tpb], signal=(k == 7))
        S.op("act", lambda s=s: nc.scalar.copy(out=memT[:, :, s * 128:(s + 1) * 128], in_=tpb[:]),
             reads=[tpb], writes=[memT])
    pb = banks[0]
    for c in range(2):
        for k in range(8):
            S.op("pe", lambda k=k, c=c: nc.tensor.matmul(
                out=pb[:, 0:256], lhsT=wkv_sb[:, k, c * 128:(c + 1) * 128], rhs=memT[:, k, :],
                start=(k == 0), stop=(k == 7)), reads=[wkv_sb, memT], writes=[pb], signal=(k == 7))
        S.op("dve", lambda c=c: nc.vector.tensor_copy(out=kT[:, c, :], in_=pb[:, 0:256]), reads=[pb], writes=[kT])
    for mc in range(2):
        for k in range(8):
            S.op("pe", lambda k=k, mc=mc: nc.tensor.matmul(
                out=pb[:, 0:256], lhsT=memT[:, k, mc * 128:(mc + 1) * 128], rhs=wkv_sb[:, k, 256:512],
                start=(k == 0), stop=(k == 7)), reads=[wkv_sb, memT], writes=[pb], signal=(k == 7))
        S.op("dve", lambda mc=mc: nc.vector.tensor_copy(out=vm[:, mc, :], in_=pb[:, 0:256]), reads=[pb], writes=[vm])
    return kT, vm


def norm_transpose(S, X, h, hT, junk, ssq, rstd, g_bc, tpb, ident, nsub):
    nc = S.nc
    for s in range(nsub):
        S.op("act", lambda s=s: nc.scalar.activation(
            out=junk[:], in_=X[:, s, :], func=AF.Square, accum_out=ssq[:, s:s + 1]),
            reads=[X], writes=[junk, ssq])
    rms_rstd(S, ssq, rstd, D, nsub)
    for s in range(nsub):
        S.op("dve", lambda s=s: nc.vector.scalar_tensor_tensor(
            out=h[:, s, :], in0=X[:, s, :], scalar=rstd[:, s:s + 1], in1=g_bc[:],
            op0=ALU.mult, op1=ALU.mult), reads=[X, rstd, g_bc], writes=[h])
    for s in range(nsub):
        for k in range(8):
            S.op("pe", lambda k=k, s=s: nc.tensor.transpose(
                out=tpb[:, k, :], in_=h[:, s, k * 128:(k + 1) * 128], identity=ident[:]),
                reads=[h, ident], writes=[tpb], signal=(k == 7))
        S.op("act", lambda s=s: nc.scalar.copy(out=hT[:, :, s * 128:(s + 1) * 128], in_=tpb[:]),
             reads=[tpb], writes=[hT])


def mem_attention(S, kT, vm, ones_bf, mqT, catT, cat_base, sbanks, pmo, pden, pTs, rden, T):
    nc = S.nc
    for hd in range(4):
        ch = hd // 2
        po = (hd % 2) * 64
        pst = sbanks[hd]
        for mc in range(2):
            S.op("pe", lambda mc=mc, pst=pst, po=po, ch=ch: nc.tensor.matmul(
                out=pst[:, mc * T:(mc + 1) * T], lhsT=kT[po:po + 64, ch, mc * 128:(mc + 1) * 128],
                rhs=mqT[po:po + 64, ch, :], start=True, stop=True),
                reads=[kT, mqT], writes=[pst], signal=(mc == 1))
    for hd in range(4):
        S.op("act", lambda hd=hd: nc.scalar.activation(
            out=pTs[hd][:, 0:2 * T], in_=sbanks[hd][:, 0:2 * T], func=AF.Exp, scale=0.125),
            reads=[sbanks[hd]], writes=[pTs[hd]])
    for hd in range(4):
        ch = hd // 2
        po = (hd % 2) * 64
        pT = pTs[hd]
        for mc in range(2):
            S.op("pe", lambda mc=mc, pT=pT, po=po, ch=ch, hd=hd: nc.tensor.matmul(
                out=pmo[po:po + 64, ch * T:(ch + 1) * T], lhsT=vm[:, mc, hd * 64:(hd + 1) * 64],
                rhs=pT[:, mc * T:(mc + 1) * T], start=(mc == 0), stop=(mc == 1)),
                reads=[vm, pT], writes=[pmo], signal=(mc == 1))
        for mc in range(2):
            S.op("pe", lambda mc=mc, pT=pT, po=po, ch=ch: nc.tensor.matmul(
                out=pden[po:po + 64, ch * T:(ch + 1) * T], lhsT=ones_bf[:, 0:64],
                rhs=pT[:, mc * T:(mc + 1) * T], start=(mc == 0), stop=(mc == 1)),
                reads=[ones_bf, pT], writes=[pden], signal=(mc == 1))
    S.op("dve", lambda: nc.vector.reciprocal(out=rden[:, 0:2 * T], in_=pden[:, 0:2 * T]), reads=[pden], writes=[rden])
    S.op("dve", lambda: nc.vector.tensor_tensor(
        out=catT[:, cat_base:cat_base + 2, :], in0=pmo[:, 0:2 * T].rearrange("p (c t) -> p c t", c=2),
        in1=rden[:, 0:2 * T].rearrange("p (c t) -> p c t", c=2), op=ALU.mult),
        reads=[pmo, rden], writes=[catT])


def out_proj_store(S, X, catT, wout_sb, po_banks, x_dst, nsub, cnt):
    nc = S.nc
    for s in range(nsub):
        for hf in range(2):
            O = po_banks[cnt["o"] % len(po_banks)]
            cnt["o"] += 1
            for fc in range(8):
                S.op("pe", lambda fc=fc, O=O: nc.tensor.matmul(
                    out=O[:, 0:512], lhsT=catT[:, fc, s * 128:(s + 1) * 128],
                    rhs=wout_sb[:, fc, hf * 512:(hf + 1) * 512], start=(fc == 0), stop=(fc == 7)),
                    reads=[catT, wout_sb], writes=[O], signal=(fc == 7))
            S.op("dve", lambda O=O: nc.vector.tensor_tensor(
                out=X[:, s, hf * 512:(hf + 1) * 512], in0=X[:, s, hf * 512:(hf + 1) * 512], in1=O[:, 0:512],
                op=ALU.add), reads=[X, O], writes=[X])
    S.dma("sp", x_dst, X[:], reads=[X])


def gmlp_phase(S, x_dram, n_tok, mem_b, g_mix, g_mem, w_kv, w_in, v_gain, w_s, b_s, w_out, T=256, x_src=None):
    nc = S.nc
    S.begin_phase()
    S._stage_i = 0
    nsub = T // 128
    ntile = n_tok // T
    GW = 768
    ident = make_identity(S)
    ones_bf = S.sbuf("ones_bf", [128, 128], BF16)
    S.op("pool", lambda: nc.gpsimd.memset(ones_bf[:], 1.0), writes=[ones_bf])
    g_bc = load_gain_bc(S, "g_bc", g_mix, D)
    vg_bc = load_gain_bc(S, "vg_bc", v_gain, GW)
    tpb = S.psum("tpb", [128, 8, 128], BF16)
    banks = [S.psum("bank%d" % i, [128, 512], F32) for i in range(7)]
    stage = [S.sbuf("stage%d" % i, [128, 1792], F32) for i in range(2)]

    kT, vm = mem_kv_prep(S, banks, tpb, ident, mem_b, g_mem, w_kv, stage)

    win_sb = S.sbuf("win_sb", [128, 8, 1792], BF16)
    wout_sb = S.sbuf("wout_sb", [128, 8, D], BF16)
    load_weight_bf16(S, win_sb, w_in, stage)
    load_weight_bf16(S, wout_sb, w_out, stage)

    ws_nat = S.sbuf("ws_nat", [128, 12, 128], F32)
    ws_bf = S.sbuf("ws_bf", [128, 12, 128], BF16)
    wsT = S.sbuf("wsT", [128, 12, 128], BF16)
    S.dma("sp", ws_nat[:], w_s.rearrange("g t s -> t g s"), writes=[ws_nat])
    S.op("pool", lambda: nc.gpsimd.affine_select(
        out=ws_bf[:], in_=ws_nat[:], pattern=[[0, 12], [-1, 128]], compare_op=ALU.is_ge, fill=0.0,
        base=0, channel_multiplier=1), reads=[ws_nat], writes=[ws_bf])
    for g8 in range(2):
        ng = 8 if g8 == 0 else 4
        for j in range(ng):
            g = g8 * 8 + j
            S.op("pe", lambda g=g, j=j: nc.tensor.transpose(
                out=tpb[:, j, :], in_=ws_bf[:, g, :], identity=ident[:]),
                reads=[ws_bf, ident], writes=[tpb], signal=(j == ng - 1))
        S.op("act", lambda g8=g8, ng=ng: nc.scalar.copy(out=wsT[:, g8 * 8:g8 * 8 + ng, :], in_=tpb[:, 0:ng, :]),
             reads=[tpb], writes=[wsT])
    bs_f = S.sbuf("bs_f", [33, 12 * 128], F32)
    bs_t = S.sbuf("bs_t", [33, 12 * 128], BF16)
    bs2 = S.sbuf("bs2", [33, 12 * 128], BF16)
    bflat = b_s.rearrange("g t -> (g t)")
    S.dma("sp", bs_f[0:1, :], bflat.partition_broadcast(1), writes=[bs_f])
    S.dma("sp", bs_f[32:33, :], bflat.partition_broadcast(1), writes=[bs_f])
    S.op("dve", lambda: nc.vector.memset(bs2[:], 0.0), writes=[bs2])
    S.op("dve", lambda: nc.vector.tensor_copy(out=bs2[0:1, :], in_=bs_f[0:1, :]), reads=[bs_f], writes=[bs2])
    S.op("dve", lambda: nc.vector.tensor_copy(out=bs_t[32:33, :], in_=bs_f[32:33, :]), reads=[bs_f], writes=[bs_t])
    S.op("dve", lambda: nc.vector.tensor_tensor(out=bs2[32:33, :], in0=bs_f[32:33, :], in1=bs_t[32:33, :],
                                                op=ALU.subtract), reads=[bs_f, bs_t], writes=[bs2])

    xt = [S.sbuf("xt%d" % i, [128, nsub, D], F32) for i in range(2)]
    hT = [S.sbuf("hT%d" % i, [128, 8, T], BF16) for i in range(2)]
    h = S.sbuf("h", [128, nsub, D], BF16)
    junk = S.sbuf("junk", [128, D], BF16)
    ssq = S.sbuf("ssq", [128, 8], F32)
    rstd = S.sbuf("rstd", [128, 8], F32)
    vssq = S.sbuf("vssq", [128, 8], F32)
    vrstd = S.sbuf("vrstd", [128, 8], F32)
    uT = S.sbuf("uT", [128, 6, T], BF16)
    mqT = S.sbuf("mqT", [128, 2, T], BF16)
    vfs = [S.sbuf("vf%d" % i, [128, GW], F32) for i in range(2)]
    vn = [S.sbuf("vn%d" % i, [128, GW], BF16) for i in range(nsub)]
    catT = S.sbuf("catT", [128, 8, T], BF16)
    pTs = [S.sbuf("pT%d" % i, [128, 2 * T], BF16) for i in range(4)]
    rden = S.sbuf("rden", [128, 2 * T], F32)
    x_v = x_dram.rearrange("(t s p) d -> t p s d", p=128, s=nsub)
    x_in = x_v if x_src is None else x_src.rearrange("(t s p) d -> t p s d", p=128, s=nsub)
    cnt = {"o": 0, "b1": 0}

    def stage_a(i):
        X = xt[i % 2]
        S.dma("sp", X[:], x_in[i], writes=[X])
        norm_transpose(S, X, h, hT[i % 2], junk, ssq, rstd, g_bc, tpb, ident, nsub)

    def stage_b(i):
        HT = hT[i % 2]
        for j in range(8):
            col = j * 128 if j < 6 else 1536 + (j - 6) * 128
            P = banks[cnt["b1"] % 2]
            cnt["b1"] += 1
            for k in range(8):
                S.op("pe", lambda k=k, P=P, col=col: nc.tensor.matmul(
                    out=P[:, 0:T], lhsT=win_sb[:, k, col:col + 128], rhs=HT[:, k, :],
                    start=(k == 0), stop=(k == 7)), reads=[win_sb, HT], writes=[P], signal=(k == 7))
            if j < 6:
                S.op("act", lambda P=P, j=j: nc.scalar.activation(
                    out=uT[:, j, :], in_=P[:, 0:T], func=AF.Gelu_apprx_tanh), reads=[P], writes=[uT])
            else:
                S.op("dve", lambda P=P, j=j: nc.vector.tensor_copy(out=mqT[:, j - 6, :], in_=P[:, 0:T]),
                     reads=[P], writes=[mqT])
        for s in range(nsub):
            PA, PB = banks[2 + 2 * (s % 2)], banks[3 + 2 * (s % 2)]
            vf = vfs[s % 2]
            for k in range(8):
                S.op("pe", lambda k=k, PA=PA: nc.tensor.matmul(
                    out=PA[:, 0:512], lhsT=HT[:, k, s * 128:(s + 1) * 128], rhs=win_sb[:, k, 768:1280],
                    start=(k == 0), stop=(k == 7)), reads=[win_sb, HT], writes=[PA], signal=(k == 7))
            for k in range(8):
                S.op("pe", lambda k=k, PB=PB: nc.tensor.matmul(
                    out=PB[:, 0:256], lhsT=HT[:, k, s * 128:(s + 1) * 128], rhs=win_sb[:, k, 1280:1536],
                    start=(k == 0), stop=(k == 7)), reads=[win_sb, HT], writes=[PB], signal=(k == 7))
            S.op("act", lambda PA=PA, vf=vf: nc.scalar.activation(out=vf[:, 0:512], in_=PA[:, 0:512],
                                                                     func=AF.Gelu_apprx_tanh), reads=[PA], writes=[vf])
            S.op("act", lambda PB=PB, vf=vf: nc.scalar.activation(out=vf[:, 512:768], in_=PB[:, 0:256],
                                                                     func=AF.Gelu_apprx_tanh), reads=[PB], writes=[vf])
            S.op("act", lambda s=s, vf=vf: nc.scalar.activation(
                out=junk[:, 0:GW], in_=vf[:], func=AF.Square, accum_out=vssq[:, s:s + 1]),
                reads=[vf], writes=[junk, vssq])
            rms_rstd(S, vssq, vrstd, GW, s + 1, c0=s)
            S.op("dve", lambda s=s, vf=vf: nc.vector.scalar_tensor_tensor(
                out=vn[s][:], in0=vf[:], scalar=vrstd[:, s:s + 1], in1=vg_bc[:],
                op0=ALU.mult, op1=ALU.mult), reads=[vf, vrstd, vg_bc], writes=[vn[s]])
        psp = [banks[4], banks[5], banks[6]]
        for c in range(nsub):
            for g in range(12):
                fc, po = g // 2, (g % 2) * 64
                b, slot = fc // 2, (fc % 2) * T + c * 128
                P = psp[b]
                S.op("pe", lambda g=g, P=P, po=po, slot=slot, c=c: nc.tensor.matmul(
                    out=P[po:po + 64, slot:slot + 128], lhsT=vn[c][:, g * 64:(g + 1) * 64], rhs=wsT[:, g, :],
                    start=True, stop=False), reads=[vn[c], wsT], writes=[P], signal=False)
                S.op("pe", lambda g=g, P=P, po=po, slot=slot: nc.tensor.matmul(
                    out=P[po:po + 64, slot:slot + 128], lhsT=ones_bf[0:33, 0:64], rhs=bs2[0:33, g * 128:(g + 1) * 128],
                    start=False, stop=True), reads=[ones_bf, bs2], writes=[P], signal=True)
        for b in range(3):
            S.op("dve", lambda b=b: nc.vector.tensor_tensor(
                out=catT[:, 2 * b:2 * b + 2, :], in0=psp[b][:, 0:2 * T].rearrange("p (f t) -> p f t", f=2),
                in1=uT[:, 2 * b:2 * b + 2, :], op=ALU.mult), reads=[psp[b], uT], writes=[catT])
        mem_attention(S, kT, vm, ones_bf, mqT, catT, 6, [banks[0], banks[1], banks[4], banks[5]], banks[2], banks[3], pTs, rden, T)

    def stage_c(i):
        out_proj_store(S, xt[i % 2], catT, wout_sb, [banks[0], banks[1]], x_v[i], nsub, cnt)

    stage_a(0)
    for i in range(ntile):
        stage_b(i)
        if i + 1 < ntile:
            stage_a(i + 1)
        stage_c(i)
    S.end_phase()


def nsa_proj_phase(S, x_dram, n_tok, mem_b, g_mix, g_mem, w_kv, w_in, cosT_d, sinT_d,
                   featT_d, vtok_d, gates_d, memoT_d, T=256):
    nc = S.nc
    S.begin_phase()
    S._stage_i = 0
    nsub = T // 128
    ntile = n_tok // T
    ident = make_identity(S)
    ones_bf = S.sbuf("ones_bf", [128, 128], BF16)
    S.op("pool", lambda: nc.gpsimd.memset(ones_bf[:], 1.0), writes=[ones_bf])
    g_bc = load_gain_bc(S, "g_bc", g_mix, D)
    tpb = S.psum("tpb", [128, 8, 128], BF16)
    banks = [S.psum("bank%d" % i, [128, 512], F32) for i in range(7)]
    stage = [S.sbuf("stage%d" % i, [128, 2596], F32) for i in range(2)]
    kT, vm = mem_kv_prep(S, banks, tpb, ident, mem_b, g_mem, w_kv, stage)

    wA = S.sbuf("wA", [128, 8, 2048], BF16)
    wR = S.sbuf("wR", [128, 8, 1536], BF16)
    wB = S.sbuf("wB", [128, 8, 548], BF16)
    mapA = [(0, 0, 768), (768, 768, 256), (1024, 1280, 256), (1280, 1792, 256), (1536, 1024, 256), (1792, 2340, 256)]
    mapB = [(0, 1536, 256), (256, 2048, 256), (512, 2304, 36)]
    for k in range(8):
        st = stage[k % 2]
        S.dma("sp", st[:], w_in[k * 128:(k + 1) * 128, :], writes=[st])
        for (d0, s0, n) in mapA:
            cast_copy(S, wA[:, k, d0:d0 + n], st[:, s0:s0 + n], [st], [wA.sub(k)])
        for (d0, s0, n) in mapA[:4]:
            src = st[:, s0:s0 + n].rearrange("p (h two d) -> p h two d", two=2, d=32)
            dst = wR[:, k, d0:d0 + n].rearrange("p (h two d) -> p h two d", two=2, d=32)
            S.op("dve", lambda src=src, dst=dst: nc.vector.tensor_scalar(
                out=dst[:, :, 0, :], in0=src[:, :, 1, :], scalar1=-1.0, scalar2=None, op0=ALU.mult),
                reads=[st], writes=[wR.sub(k)])
            S.op("pool", lambda src=src, dst=dst: nc.gpsimd.tensor_copy(out=dst[:, :, 1, :], in_=src[:, :, 0, :]),
                 reads=[st], writes=[wR.sub(k)])
        for (d0, s0, n) in mapB:
            cast_copy(S, wB[:, k, d0:d0 + n], st[:, s0:s0 + n], [st], [wB.sub(k)])

    xt = [S.sbuf("xt%d" % i, [128, nsub, D], F32) for i in range(2)]
    hT = [S.sbuf("hT%d" % i, [128, 8, T], BF16) for i in range(2)]
    h = S.sbuf("h", [128, nsub, D], BF16)
    junk = S.sbuf("junk", [128, D], BF16)
    ssq = S.sbuf("ssq", [128, 8], F32)
    rstd = S.sbuf("rstd", [128, 8], F32)
    cs = [S.sbuf("cos%d" % i, [128, T], F32) for i in range(2)]
    sn = [S.sbuf("sin%d" % i, [128, T], F32) for i in range(2)]
    t1s = [S.sbuf("t1_%d" % i, [128, T], F32) for i in range(2)]
    t2s = [S.sbuf("t2_%d" % i, [128, T], F32) for i in range(2)]
    featT = S.sbuf("featT", [128, 14, T], BF16)
    mqT = S.sbuf("mqT", [128, 2, T], BF16)
    memoT = S.sbuf("memoT", [128, 2, T], BF16)
    vtok = S.sbuf("vtok", [128, nsub, 512], BF16)
    gsb = S.sbuf("gsb", [128, nsub, 36], F32)
    pTs = [S.sbuf("pT%d" % i, [128, 2 * T], BF16) for i in range(4)]
    rden = S.sbuf("rden", [128, 2 * T], F32)
    x_v = x_dram.rearrange("(t s p) d -> t p s d", p=128, s=nsub)
    feat_v = featT_d.rearrange("(c p) s -> p c s", p=128)
    memo_v = memoT_d.rearrange("(c p) s -> p c s", p=128)
    vtok_v = vtok_d.rearrange("(t s p) n -> t p s n", p=128, s=nsub)
    gates_v = gates_d.rearrange("(t s p) n -> t p s n", p=128, s=nsub)
    cnt = {"b": 0}

    hs = [h, S.sbuf("h_b", [128, nsub, D], BF16)]

    def stage_a1(i):
        X = xt[i % 2]
        S.dma("sp", X[:], x_v[i], writes=[X])
        S.dma("sp", cs[i % 2][:], cosT_d[:, i * T:(i + 1) * T], writes=[cs[i % 2]])
        S.dma("sp", sn[i % 2][:], sinT_d[:, i * T:(i + 1) * T], writes=[sn[i % 2]])
        hh = hs[i % 2]
        for s in range(nsub):
            S.op("act", lambda s=s: nc.scalar.activation(
                out=junk[:], in_=X[:, s, :], func=AF.Square, accum_out=ssq[:, s:s + 1]),
                reads=[X], writes=[junk, ssq])
        rms_rstd(S, ssq, rstd, D, nsub)
        for s in range(nsub):
            S.op("dve", lambda s=s: nc.vector.scalar_tensor_tensor(
                out=hh[:, s, :], in0=X[:, s, :], scalar=rstd[:, s:s + 1], in1=g_bc[:],
                op0=ALU.mult, op1=ALU.mult), reads=[X, rstd, g_bc], writes=[hh])

    def stage_a2(i):
        hh = hs[i % 2]
        HT = hT[i % 2]
        for s in range(nsub):
            for k in range(8):
                S.op("pe", lambda k=k, s=s: nc.tensor.transpose(
                    out=tpb[:, k, :], in_=hh[:, s, k * 128:(k + 1) * 128], identity=ident[:]),
                    reads=[hh, ident], writes=[tpb], signal=(k == 7))
            S.op("act", lambda s=s: nc.scalar.copy(out=HT[:, :, s * 128:(s + 1) * 128], in_=tpb[:]),
                 reads=[tpb], writes=[HT])

    def proj(P, wsb, col, HT):
        for k in range(8):
            S.op("pe", lambda k=k: nc.tensor.matmul(
                out=P[:, 0:T], lhsT=wsb[:, k, col:col + 128], rhs=HT[:, k, :],
                start=(k == 0), stop=(k == 7)), reads=[wsb, HT], writes=[P], signal=(k == 7))

    def stage_b(i):
        HT = hT[i % 2]
        CS, SN = cs[i % 2], sn[i % 2]
        for c in range(16):
            Pa = banks[(cnt["b"] % 2) * 2]
            Pb = banks[(cnt["b"] % 2) * 2 + 1]
            cnt["b"] += 1
            proj(Pa, wA, c * 128, HT)
            if c < 12:
                t1, t2 = t1s[c % 2], t2s[c % 2]
                proj(Pb, wR, c * 128, HT)
                S.op("dve", lambda Pa=Pa: nc.vector.tensor_tensor(out=t1[:], in0=Pa[:, 0:T], in1=CS[:], op=ALU.mult),
                     reads=[Pa, CS], writes=[t1])
                S.op("dve", lambda Pb=Pb: nc.vector.tensor_tensor(out=t2[:], in0=Pb[:, 0:T], in1=SN[:], op=ALU.mult),
                     reads=[Pb, SN], writes=[t2])
                S.op("pool", lambda c=c: nc.gpsimd.tensor_tensor(out=featT[:, c, :], in0=t1[:], in1=t2[:], op=ALU.add),
                     reads=[t1, t2], writes=[featT.sub(c)])
            elif c < 14:
                S.op("act", lambda Pa=Pa, c=c: nc.scalar.copy(out=featT[:, c, :], in_=Pa[:, 0:T]),
                     reads=[Pa], writes=[featT.sub(c)])
            else:
                S.op("act", lambda Pa=Pa, c=c: nc.scalar.copy(out=mqT[:, c - 14, :], in_=Pa[:, 0:T]),
                     reads=[Pa], writes=[mqT])
        for s in range(nsub):
            PA, PB = banks[4], banks[5]
            for k in range(8):
                S.op("pe", lambda k=k: nc.tensor.matmul(
                    out=PA[:, 0:512], lhsT=HT[:, k, s * 128:(s + 1) * 128], rhs=wB[:, k, 0:512],
                    start=(k == 0), stop=(k == 7)), reads=[wB, HT], writes=[PA], signal=(k == 7))
            for k in range(8):
                S.op("pe", lambda k=k: nc.tensor.matmul(
                    out=PB[:, 0:36], lhsT=HT[:, k, s * 128:(s + 1) * 128], rhs=wB[:, k, 512:548],
                    start=(k == 0), stop=(k == 7)), reads=[wB, HT], writes=[PB], signal=(k == 7))
            S.op("act", lambda s=s: nc.scalar.copy(out=vtok[:, s, :], in_=PA[:, 0:512]), reads=[PA], writes=[vtok])
            S.op("act", lambda s=s: nc.scalar.activation(out=gsb[:, s, :], in_=PB[:, 0:36], func=AF.Exp, scale=-1.0),
                 reads=[PB], writes=[gsb])
            S.op("dve", lambda s=s: nc.vector.tensor_scalar(
                out=gsb[:, s, :], in0=gsb[:, s, :], scalar1=1.0, scalar2=None, op0=ALU.add), reads=[gsb], writes=[gsb])
            S.op("dve", lambda s=s: nc.vector.reciprocal(out=gsb[:, s, :], in_=gsb[:, s, :]), reads=[gsb], writes=[gsb])
        mem_attention(S, kT, vm, ones_bf, mqT, memoT, 0, [banks[0], banks[1], banks[2], banks[3]], banks[4], banks[5], pTs, rden, T)
        S.dma("sp", feat_v[:, 0:14, i * T:(i + 1) * T], featT[:], reads=[featT])
        S.dma("sp", memo_v[:, :, i * T:(i + 1) * T], memoT[:], reads=[memoT])
        S.dma("sp", vtok_v[i], vtok[:], reads=[vtok])
        S.dma("sp", gates_v[i], gsb[:], reads=[gsb])

    stage_a1(0)
    stage_a2(0)
    for i in range(ntile):
        if i + 1 < ntile:
            stage_a1(i + 1)
        stage_b(i)
        if i + 1 < ntile:
            stage_a2(i + 1)
    S.end_phase()


def nsa_compress_phase(S, n_tok, featT_d, pe_k, pe_v, ck_w1, ck_w2, cv_w1, cv_w2, kcT_d, vc_d):
    nc = S.nc
    S.begin_phase()
    NC_ = n_tok // 16 - 1
    ncc = (NC_ + 127) // 128
    NCP = ncc * 128
    banks = [S.psum("bank%d" % i, [128, 512], F32) for i in range(4)]
    kin = S.sbuf("kin", [128, n_tok], BF16)
    w1f = S.sbuf("w1f", [128, 32, 256], F32)
    w1 = S.sbuf("w1", [128, 32, 256], BF16)
    w2f = S.sbuf("w2f", [128, 2, 64], F32)
    w2 = S.sbuf("w2", [128, 2, 64], BF16)
    pef = S.sbuf("pef", [128, 32], F32)
    peb = S.sbuf("peb", [128, 32], BF16)
    bias = S.sbuf("bias", [128, 2], F32)
    hid = S.sbuf("hid", [128, 2, NCP], BF16)
    outk = S.sbuf("outk", [128, NCP], BF16)
    outv = S.sbuf("outv", [128, ncc, 64], BF16)
    S.op("pool", lambda: nc.gpsimd.memset(hid[:], 0.0), writes=[hid])
    S.op("pool", lambda: nc.gpsimd.memset(outk[:], 0.0), writes=[outk])
    for kind in range(2):
        w1_d, w2_d, pe_d = (ck_w1, ck_w2, pe_k) if kind == 0 else (cv_w1, cv_w2, pe_v)
        w1v = w1_d.rearrange("(j d) n -> d j n", d=64)
        for half in range(2):
            S.dma("sp", w1f[half * 64:(half + 1) * 64, :, :], w1v, writes=[w1f])
            S.dma("sp", pef[half * 64:(half + 1) * 64, :], pe_d.rearrange("j d -> d j"), writes=[pef],
                  allow_slow_non_contiguous=True)
        S.dma("sp", w2f[:], w2_d.rearrange("(c p) n -> p c n", p=128), writes=[w2f])
        S.op("dve", lambda: nc.vector.tensor_copy(out=w1[:, 0:16, :], in_=w1f[:, 0:16, :]), reads=[w1f], writes=[w1])
        S.op("pool", lambda: nc.gpsimd.tensor_copy(out=w1[:, 16:32, :], in_=w1f[:, 16:32, :]), reads=[w1f], writes=[w1])
        S.op("dve", lambda: nc.vector.tensor_copy(out=w2[:], in_=w2f[:]), reads=[w2f], writes=[w2])
        S.op("dve", lambda: nc.vector.tensor_copy(out=peb[:], in_=pef[:]), reads=[pef], writes=[peb])
        for hc in range(2):
            for j in range(32):
                S.op("pe", lambda j=j, hc=hc: nc.tensor.matmul(
                    out=banks[3][:, hc:hc + 1], lhsT=w1[0:64, j, hc * 128:(hc + 1) * 128], rhs=peb[0:64, j:j + 1],
                    start=(j == 0), stop=(j == 31)), reads=[w1, peb], writes=[banks[3]], signal=(j == 31))
        S.op("dve", lambda: nc.vector.tensor_copy(out=bias[:], in_=banks[3][:, 0:2]), reads=[banks[3]], writes=[bias])
        for ch in range(2):
            row0 = (768 if kind == 0 else 1536) + ch * 128
            S.dma("sp", kin[:], featT_d[row0:row0 + 128, :], writes=[kin])
            for hh in range(2):
                g = ch * 2 + hh
                po = hh * 64
                for hc in range(2):
                    P = banks[hc]
                    for j in range(32):
                        S.op("pe", lambda j=j, hc=hc, P=P: nc.tensor.matmul(
                            out=P[:, 0:NC_], lhsT=w1[po:po + 64, j, hc * 128:(hc + 1) * 128],
                            rhs=kin[po:po + 64, j:j + 16 * (NC_ - 1) + 1:16],
                            start=(j == 0), stop=(j == 31)), reads=[w1, kin], writes=[P], signal=(j == 31))
                    S.op("act", lambda hc=hc, P=P: nc.scalar.activation(
                        out=hid[:, hc, 0:NC_], in_=P[:, 0:NC_], func=AF.Silu, bias=bias[:, hc:hc + 1]),
                        reads=[P, bias], writes=[hid])
                if kind == 0:
                    P = banks[2]
                    for hc in range(2):
                        S.op("pe", lambda hc=hc: nc.tensor.matmul(
                            out=P[0:64, 0:NCP], lhsT=w2[:, hc, 0:64], rhs=hid[:, hc, :],
                            start=(hc == 0), stop=(hc == 1)), reads=[w2, hid], writes=[P], signal=(hc == 1))
                    S.op("dve", lambda: nc.vector.tensor_copy(out=outk[0:64, 0:NC_], in_=P[0:64, 0:NC_]),
                         reads=[P], writes=[outk])
                    S.dma("sp", kcT_d[g * 64:(g + 1) * 64, :], outk[0:64, :], reads=[outk])
                else:
                    P = banks[2]
                    for cc in range(ncc):
                        for hc in range(2):
                            S.op("pe", lambda hc=hc, cc=cc: nc.tensor.matmul(
                                out=P[:, cc * 64:(cc + 1) * 64], lhsT=hid[:, hc, cc * 128:(cc + 1) * 128],
                                rhs=w2[:, hc, 0:64], start=(hc == 0), stop=(hc == 1)),
                                reads=[w2, hid], writes=[P], signal=(hc == 1))
                    S.op("dve", lambda: nc.vector.tensor_copy(
                        out=outv[:], in_=P[:, 0:ncc * 64].rearrange("p (c d) -> p c d", d=64)), reads=[P], writes=[outv])
                    S.dma("sp", vc_d[g].rearrange("(c p) d -> p c d", p=128), outv[:], reads=[outv])
    S.end_phase()


def nsa_attn_phase(S, x_dram, n_tok, featT_d, vtok_d, gates_d, memoT_d, kcT_d, vc_d, selmap_d, bq_d, w_out):
    nc = S.nc
    S.begin_phase()
    S._stage_i = 0
    NT = n_tok // 128
    NB = n_tok // 64
    NC_ = n_tok // 16 - 1
    ncc_tot = (NC_ + 127) // 128
    NCP = ncc_tot * 128
    nhalf = (NB + 63) // 64
    NBA = min(NB, 64)
    TINY = 1e-30
    identf = make_identity(S, "identf", F32)
    sp = [S.psum("sp%d" % i, [128, 1024], F32) for i in range(2)]
    bS = S.psum("bS", [128, 512], F32)
    bC = S.psum("bC", [128, 512], F32)
    bI = S.psum("bI", [128, 512], F32)
    b7 = S.psum("b7", [128, 512], F32)
    stage = [S.sbuf("stage%d" % i, [128, 512], F32) for i in range(2)]
    wout_sb = S.sbuf("wout_sb", [128, 8, D], BF16)
    load_weight_bf16(S, wout_sb, w_out, stage, col_lo=0, col_hi=512, dst_lo=0)
    load_weight_bf16(S, wout_sb, w_out, stage, col_lo=512, col_hi=1024, dst_lo=512)
    zeros = S.sbuf("zeros", [128, 512], BF16)
    S.op("pool", lambda: nc.gpsimd.memset(zeros[:], 0.0), writes=[zeros])
    ident_bf = make_identity(S, "ident_bf", BF16)
    NEGB = nc.gpsimd.to_reg(-30000.0)
    z3 = zeros[:, 0:384].rearrange("p (r q) -> p r q", r=3)

    def make_bias(dst_ap, dst_buf, pattern, cm, base):
        S.op("pool", lambda: nc.gpsimd.affine_select(
            out=dst_ap, in_=z3, pattern=pattern, compare_op=ALU.is_ge, fill=NEGB, base=base, channel_multiplier=cm),
            reads=[zeros], writes=[dst_buf])

    Mdiag = S.sbuf("Mdiag", [128, 384], BF16)
    Mfar = S.sbuf("Mfar", [128, 384], BF16)
    make_bias(Mdiag[:].rearrange("p (r q) -> p r q", r=3), Mdiag, [[0, 3], [1, 128]], -1, 0)
    make_bias(Mfar[:].rearrange("p (r q) -> p r q", r=3), Mfar, [[0, 3], [-1, 128]], 1, -1)
    Mc = [S.sbuf("Mc%d" % i, [128, 2, 384], BF16) for i in range(2)]

    ksE = S.sbuf("ksE", [128, 4, n_tok], BF16)
    piece = min(2048, n_tok)
    c30k = S.sbuf("c30k", [128, piece], BF16)
    S.op("pool", lambda: nc.gpsimd.memset(c30k[:], 30000.0), writes=[c30k])
    for g in range(4):
        for p0 in range(0, n_tok, piece):
            S.op("pool", lambda g=g, p0=p0: nc.gpsimd.affine_select(
                out=ksE[64:128, g, p0:p0 + piece], in_=c30k[64:128, 0:piece],
                pattern=[[1, piece // 64], [0, 64]], compare_op=ALU.is_equal, fill=0.0,
                base=(p0 // 64) % 64, channel_multiplier=-1), reads=[c30k], writes=[ksE])
    vs_ext = S.sbuf("vs_ext", [128, NT, 4, 65], BF16)
    S.op("pool", lambda: nc.gpsimd.memset(vs_ext[:], 1.0), writes=[vs_ext])
    kw_ring = S.sbuf("kw_ring", [128, 6, 4, 128], BF16)
    vw_ring = S.sbuf("vw_ring", [128, 6, 4, 65], BF16)
    S.op("pool", lambda: nc.gpsimd.memset(vw_ring[:], 1.0), writes=[vw_ring])
    kcT_sb = S.sbuf("kcT_sb", [128, 4, NCP], BF16)
    S.dma("sp", kcT_sb[0:64, :, :], kcT_d.rearrange("(g d) c -> d g c", d=64), writes=[kcT_sb])
    vcR = S.sbuf("vcR", [128, ncc_tot, 4, 65 + NB], BF16)
    S.op("pool", lambda: nc.gpsimd.memset(vcR[:], 1.0), writes=[vcR])
    smf = S.sbuf("smf", [128, ncc_tot, NB], F32)
    S.dma("sp", smf[:], selmap_d.rearrange("(c p) n -> p c n", p=128), writes=[smf])
    for g in range(4):
        S.dma("sp", vcR[:, :, g, 0:64], vc_d[g].rearrange("(c p) d -> p c d", p=128), writes=[vcR])
        S.op("dve", lambda g=g: nc.vector.tensor_copy(out=vcR[:, :, g, 65:65 + NB], in_=smf[:]),
             reads=[smf], writes=[vcR])

    xt = [S.sbuf("xt%d" % i, [128, D], F32) for i in range(2)]
    qA = [S.sbuf("qA%d" % i, [128, 12, 128], BF16) for i in range(2)]
    qB = [S.sbuf("qB%d" % i, [128, 12, 128], BF16) for i in range(2)]
    gts = [S.sbuf("gts%d" % i, [128, 36], F32) for i in range(2)]
    bqs = [S.sbuf("bqs%d" % i, [128, NB], F32) for i in range(2)]
    catT = [S.sbuf("catT%d" % i, [128, 8, 128], BF16) for i in range(2)]
    NPB = 4
    pbuf = [S.sbuf("pbuf%d" % i, [128, 2, 384], BF16) for i in range(NPB)]
    raw = [S.sbuf("raw%d" % i, [128, 12, 65], F32) for i in range(3)]
    rdc = S.sbuf("rdc", [128, 4], F32)
    imp = S.sbuf("imp", [128, NB], F32)
    imp2 = S.sbuf("imp2", [128, NB], F32)
    v8a = S.sbuf("v8a", [128, 8], F32)
    v8b = S.sbuf("v8b", [128, 8], F32)
    selmA_g = [S.sbuf("selmA%d" % i, [128, 128], F32) for i in range(4)]
    selmN_g = [S.sbuf("selmN%d" % i, [128, 128], F32) for i in range(4)]
    for t in selmA_g + selmN_g:
        S.op("dve", lambda t=t: nc.vector.memset(t[:], 0.0), writes=[t])
    den = S.sbuf("den", [128, 12], F32)
    coef = S.sbuf("coef", [128, 12], F32)
    o_acc = S.sbuf("o_acc", [128, 12, 64], F32)
    o_tmp = S.sbuf("o_tmp", [128, 12, 64], F32)
    for t in qA + qB:
        S.op("dve", lambda t=t: nc.vector.memset(t[:], 0.0), writes=[t])

    feat2 = featT_d
    x_v = x_dram.rearrange("(t p) d -> t p d", p=128)
    memo_v = memoT_d.rearrange("(c p) s -> p c s", p=128)
    cnt = {"s": 0, "p": 0}

    def load_tile(i):
        t0 = i * 128
        k = i % 2
        S.dma_group_begin()
        _load_tile(i, t0, k)
        S.dma_group_end()

    def _load_tile(i, t0, k):
        S.dma("sp", xt[k][:], x_v[i], writes=[xt[k]])
        qsrc = feat2[0:768, t0:t0 + 128].rearrange("(h d) s -> d h s", d=64)
        S.dma("sp", qA[k][0:64, :, :], qsrc, writes=[qA[k]], sem_buf=xt[k])
        if nhalf == 2:
            S.dma("sp", qB[k][0:64, :, :], qsrc, writes=[qB[k]], sem_buf=xt[k])
        S.dma("sp", gts[k][:], gates_d[t0:t0 + 128, :], writes=[gts[k]], sem_buf=xt[k])
        S.dma("sp", bqs[k][:], bq_d[t0:t0 + 128, :], writes=[bqs[k]], sem_buf=xt[k])
        S.dma("sp", catT[k][:, 6:8, :], memo_v[:, :, t0:t0 + 128], writes=[catT[k]], sem_buf=xt[k])
        S.dma("sp", ksE[0:64, :, t0:t0 + 128], feat2[1024:1280, t0:t0 + 128].rearrange("(g d) s -> d g s", d=64),
              writes=[ksE.sub(i)], sem_buf=xt[k])
        S.dma("sp", vs_ext[:, i, :, 0:64], vtok_d[t0:t0 + 128, 0:256].rearrange("p (g d) -> p g d", d=64),
              writes=[vs_ext.sub(i)], sem_buf=xt[k])
        sl = i % 6
        S.dma("sp", kw_ring[0:64, sl, :, :], feat2[1280:1536, t0:t0 + 128].rearrange("(g d) s -> d g s", d=64),
              writes=[kw_ring.sub(sl)], sem_buf=xt[k])
        S.dma("sp", vw_ring[:, sl, :, 0:64], vtok_d[t0:t0 + 128, 256:512].rearrange("p (g d) -> p g d", d=64),
              writes=[vw_ring.sub(sl)], sem_buf=xt[k])

    stream = []

    def attend(tiles, outs, pre=None, post=None):
        us = [tiles[u:u + 2] for u in range(0, len(tiles), 2)]
        for ui, tl in enumerate(us):
            stream.append(dict(tiles=tl, outs=outs, first=(ui == 0), last=(ui == len(us) - 1),
                               pre=(pre if ui == 0 else None), post=(post if ui == len(us) - 1 else None)))

    def run_stream():
        n = len(stream)
        spb = {}
        pbs = {}

        def do_score(u):
            U = stream[u]
            if U["pre"] is not None:
                U["pre"]()
            SPb = sp[u % 2]
            nt = len(U["tiles"])
            for t, tl in enumerate(U["tiles"]):
                bias = tl.get("bias", [])
                S.op("pe", lambda t=t, tl=tl, SPb=SPb, bias=bias: nc.tensor.matmul(
                    out=SPb[:, t * 512:t * 512 + 384], lhsT=tl["lhsT"], rhs=tl["rhs"], start=True,
                    stop=(len(bias) == 0)), reads=tl["reads"], writes=[SPb],
                    signal=(t == nt - 1 and len(bias) == 0))
                for bi_, (bap, breads) in enumerate(bias):
                    S.op("pe", lambda t=t, SPb=SPb, bap=bap, bi_=bi_, bias=bias: nc.tensor.matmul(
                        out=SPb[:, t * 512:t * 512 + 384], lhsT=ident_bf[:], rhs=bap, start=False,
                        stop=(bi_ == len(bias) - 1)), reads=[ident_bf] + breads, writes=[SPb],
                        signal=(t == nt - 1 and bi_ == len(bias) - 1))
            spb[u] = SPb

        def do_exp(u):
            U = stream[u]
            SPb = spb.pop(u)
            nt = len(U["tiles"])
            PB = pbuf[u % NPB]
            pbs[u] = PB
            S.op("act", lambda: nc.scalar.activation(
                out=PB[:, 0:nt, :], in_=SPb[:].rearrange("p (t c) -> p t c", t=2)[:, 0:nt, 0:384],
                func=AF.Exp, scale=0.125), reads=[SPb], writes=[PB])

        def do_pv(u):
            U = stream[u]
            PB = pbs.pop(u)
            outs = U["outs"]
            nt = len(U["tiles"])
            if U["first"]:
                for (bank, ncols) in outs:
                    S.op("pe", lambda bank=bank, ncols=ncols: nc.tensor.matmul(
                        out=bank[:, 0:3 * ncols], lhsT=zeros[:, 0:128], rhs=zeros[:, 0:3 * ncols],
                        start=True, stop=False), reads=[zeros], writes=[bank], signal=False)
            for t, tl in enumerate(U["tiles"]):
                fin = U["last"] and t == nt - 1
                for oi, (bank, ncols) in enumerate(outs):
                    rhs_ap, rreads = tl["pv"][oi]
                    for r in range(3):
                        S.op("pe", lambda t=t, r=r, bank=bank, ncols=ncols, rhs_ap=rhs_ap, fin=fin: nc.tensor.matmul(
                            out=bank[:, r * ncols:(r + 1) * ncols], lhsT=PB[:, t, r * 128:(r + 1) * 128], rhs=rhs_ap,
                            start=False, stop=(fin and r == 2)), reads=[PB] + rreads, writes=[bank],
                            signal=(r == 2 and (fin or oi == len(outs) - 1)))

        def do_post(u):
            if u >= 0 and stream[u]["post"] is not None:
                stream[u]["post"]()

        do_score(0)
        if n > 1:
            do_score(1)
        for u in range(n):
            do_exp(u)
            do_post(u - 1)
            if u + 2 < n:
                do_score(u + 2)
            do_pv(u)
        do_post(n - 1)

    MROW = [[0, 3], [1, 128]]

    def comp(i, g):
        k = i % 2
        QA = qA[k]
        ncc_i = min(ncc_tot, (8 * i + 6) // 128 + 1)
        tiles = []
        partial = [cc for cc in range(ncc_i) if 128 * i - 2048 * cc - 31 < 2032]
        assert len(partial) <= 2
        for cc in range(ncc_i):
            bias = []
            if cc in partial:
                bias = [(Mc[k][:, partial.index(cc), :], [Mc[k]])]
            tiles.append(dict(
                lhsT=kcT_sb[0:64, g, cc * 128:(cc + 1) * 128], rhs=QA[0:64, 3 * g:3 * g + 3, :],
                reads=[kcT_sb, QA.sub(g)], bias=bias,
                pv=[(vcR[:, cc, g, 0:65], [vcR]), (vcR[:, cc, g, 65:65 + NB], [vcR])]))

        def pre():
            for pi, cc in enumerate(partial):
                make_bias(Mc[k][:, pi, :].rearrange("p (r q) -> p r q", r=3), Mc[k], [[0, 3], [1, 128]], -16,
                          128 * i - 2048 * cc - 31)
        attend(tiles, [(bC, 65), (bI, NB)], pre=(pre if g == 0 else None), post=lambda: comp_post(i, g))

    def comp_post(i, g):
        k = i % 2
        selmA, selmN = selmA_g[g], selmN_g[g]
        S.op("act", lambda: nc.scalar.copy(
            out=raw[0][:, 3 * g:3 * g + 3, :], in_=bC[:, 0:195].rearrange("p (r e) -> p r e", e=65)),
            reads=[bC], writes=[raw[0].sub(g)])
        S.op("dve", lambda: nc.vector.tensor_scalar(
            out=rdc[:, 0:3], in0=raw[0][:, 3 * g:3 * g + 3, 64], scalar1=TINY, scalar2=None, op0=ALU.max),
            reads=[raw[0].sub(g)], writes=[rdc])
        S.op("dve", lambda: nc.vector.reciprocal(out=rdc[:, 0:3], in_=rdc[:, 0:3]), reads=[rdc], writes=[rdc])
        S.op("dve", lambda: nc.vector.scalar_tensor_tensor(
            out=imp[:], in0=bI[:, 0:NB], scalar=rdc[:, 0:1], in1=bqs[k][:], op0=ALU.mult, op1=ALU.add),
            reads=[bI, rdc, bqs[k]], writes=[imp])
        for r in (1, 2):
            S.op("dve", lambda r=r: nc.vector.scalar_tensor_tensor(
                out=imp[:], in0=bI[:, r * NB:(r + 1) * NB], scalar=rdc[:, r:r + 1], in1=imp[:],
                op0=ALU.mult, op1=ALU.add), reads=[bI, rdc, imp], writes=[imp])
        S.op("dve", lambda: nc.vector.max(out=v8a[:], in_=imp[:]), reads=[imp], writes=[v8a])
        S.op("dve", lambda: nc.vector.match_replace(out=imp2[:], in_to_replace=v8a[:], in_values=imp[:],
                                                    imm_value=-3.0e38), reads=[imp, v8a], writes=[imp2])
        S.op("dve", lambda: nc.vector.max(out=v8b[:], in_=imp2[:]), reads=[imp2], writes=[v8b])
        S.op("dve", lambda: nc.vector.tensor_scalar(
            out=selmA[:, 64:64 + NBA], in0=imp[:, 0:NBA], scalar1=v8b[:, 7:8], scalar2=-1.0,
            op0=ALU.is_ge, op1=ALU.add), reads=[imp, v8b], writes=[selmA])
        if nhalf == 2:
            S.op("dve", lambda: nc.vector.tensor_scalar(
                out=selmN[:, 0:NB], in0=imp[:, 0:NB], scalar1=v8b[:, 7:8], scalar2=-1.0,
                op0=ALU.is_ge, op1=ALU.add), reads=[imp, v8b], writes=[selmN])

    def selT(i, g):
        k = i % 2
        selmA, selmN = selmA_g[g], selmN_g[g]
        S.op("pe", lambda: nc.tensor.transpose(out=bS[:, 256:384], in_=selmA[:], identity=identf[:]),
             reads=[selmA, identf], writes=[bS])
        if nhalf == 2:
            S.op("pe", lambda: nc.tensor.transpose(out=bS[:, 384:512], in_=selmN[:], identity=identf[:]),
                 reads=[selmN, identf], writes=[bS])
        S.op("act", lambda: nc.scalar.copy(
            out=qA[k][64:128, 3 * g:3 * g + 3, :],
            in_=bS[64:128, 256:384].unsqueeze(1).to_broadcast([64, 3, 128])), reads=[bS], writes=[qA[k].sub(g)])
        if nhalf == 2:
            S.op("act", lambda: nc.scalar.copy(
                out=qB[k][64:128, 3 * g:3 * g + 3, :],
                in_=bS[64:128, 384:512].unsqueeze(1).to_broadcast([64, 3, 128])), reads=[bS], writes=[qB[k].sub(g)])

    def sel(i, g):
        k = i % 2
        tiles = []
        for j in range(i + 1):
            Q = qA[k] if j < 32 else qB[k]
            tiles.append(dict(
                lhsT=ksE[:, g, j * 128:(j + 1) * 128], rhs=Q[:, 3 * g:3 * g + 3, :], reads=[ksE.sub(j), Q.sub(g)],
                bias=([(Mdiag[:], [Mdiag])] if j == i else []), pv=[(vs_ext[:, j, g, :], [vs_ext.sub(j)])]))
        def post():
            S.op("act", lambda: nc.scalar.copy(
                out=raw[1][:, 3 * g:3 * g + 3, :], in_=bS[:, 0:195].rearrange("p (r e) -> p r e", e=65)),
                reads=[bS], writes=[raw[1].sub(g)])
            if g == 3:
                finish(i)
                if i == NT - 1:
                    finish_b(i)

        def pre():
            if g == 1 and i > 0:
                finish_b(i - 1)
                if i + 1 < NT:
                    load_tile(i + 1)
            selT(i, g)
        attend(tiles, [(bS, 65)], pre=pre, post=post)

    def win(i, g):
        k = i % 2
        tiles = []
        for j in range(max(0, i - 4), i + 1):
            sl = j % 6
            masks = []
            if j == i:
                masks.append((Mdiag[:], [Mdiag]))
            if j == i - 4:
                masks.append((Mfar[:], [Mfar]))
            tiles.append(dict(
                lhsT=kw_ring[0:64, sl, g, :], rhs=qA[k][0:64, 3 * g:3 * g + 3, :],
                reads=[kw_ring.sub(sl), qA[k].sub(g)], bias=masks, pv=[(vw_ring[:, sl, g, :], [vw_ring.sub(sl)])]))
        attend(tiles, [(b7, 65)], post=lambda: S.op("act", lambda: nc.scalar.copy(
            out=raw[2][:, 3 * g:3 * g + 3, :], in_=b7[:, 0:195].rearrange("p (r e) -> p r e", e=65)),
            reads=[b7], writes=[raw[2].sub(g)]))

    def finish(i):
        k = i % 2
        gv = gts[k][:].rearrange("p (h b) -> p h b", b=3)
        for bi in range(3):
            S.op("dve", lambda bi=bi: nc.vector.tensor_scalar(
                out=den[:], in0=raw[bi][:, :, 64], scalar1=TINY, scalar2=None, op0=ALU.max),
                reads=[raw[bi]], writes=[den])
            S.op("dve", lambda: nc.vector.reciprocal(out=den[:], in_=den[:]), reads=[den], writes=[den])
            S.op("dve", lambda bi=bi: nc.vector.tensor_tensor(out=coef[:], in0=den[:], in1=gv[:, :, bi], op=ALU.mult),
                 reads=[den, gts[k]], writes=[coef])
            dst = o_acc if bi == 0 else o_tmp
            S.op("dve", lambda bi=bi, dst=dst: nc.vector.tensor_tensor(
                out=dst[:], in0=raw[bi][:, :, 0:64], in1=coef[:].unsqueeze(2).to_broadcast([128, 12, 64]),
                op=ALU.mult), reads=[raw[bi], coef], writes=[dst])
            if bi > 0:
                S.op("dve", lambda: nc.vector.tensor_tensor(out=o_acc[:], in0=o_acc[:], in1=o_tmp[:], op=ALU.add),
                     reads=[o_acc, o_tmp], writes=[o_acc])

    def finish_b(i):
        k = i % 2
        oflat = o_acc[:].rearrange("p h d -> p (h d)")
        for half in range(2):
            PB = bS
            for c in range(3):
                fc = half * 3 + c
                S.op("pe", lambda fc=fc, c=c, PB=PB: nc.tensor.transpose(
                    out=PB[:, c * 128:(c + 1) * 128], in_=oflat[:, fc * 128:(fc + 1) * 128], identity=identf[:]),
                    reads=[o_acc, identf], writes=[PB], signal=(c == 2))
            S.op("act", lambda half=half, PB=PB: nc.scalar.copy(
                out=catT[k][:, half * 3:half * 3 + 3, :], in_=PB[:, 0:384].rearrange("p (c q) -> p c q", c=3)),
                reads=[PB], writes=[catT[k]])
        X = xt[k]
        for hf in range(2):
            O = bS
            for fc in range(8):
                S.op("pe", lambda fc=fc, O=O: nc.tensor.matmul(
                    out=O[:, 0:512], lhsT=catT[k][:, fc, :], rhs=wout_sb[:, fc, hf * 512:(hf + 1) * 512],
                    start=(fc == 0), stop=(fc == 7)), reads=[catT[k], wout_sb], writes=[O], signal=(fc == 7))
            S.op("dve", lambda O=O, hf=hf: nc.vector.tensor_tensor(
                out=X[:, hf * 512:(hf + 1) * 512], in0=X[:, hf * 512:(hf + 1) * 512], in1=O[:, 0:512], op=ALU.add),
                reads=[X, O], writes=[X])
        S.dma("sp", x_v[i], X[:], reads=[X])

    load_tile(0)
    if NT > 1:
        load_tile(1)
    for i in range(NT):
        if i < 2:
            comp(i, 0)
            win(i, 0)
            for g in range(4):
                if g + 1 < 4:
                    comp(i, g + 1)
                sel(i, g)
                if g + 1 < 4:
                    win(i, g + 1)
        else:
            comp(i, 0)
            comp(i, 1)
            win(i, 0)
            sel(i, 0)
            comp(i, 2)
            win(i, 1)
            sel(i, 1)
            comp(i, 3)
            win(i, 2)
            sel(i, 2)
            win(i, 3)
            sel(i, 3)
    run_stream()
    S.end_phase()


def host_constants(n_tok):
    half = 32
    pos = np.arange(n_tok, dtype=np.float32)
    inv = (np.float32(10000.0) ** (-np.arange(half, dtype=np.float32) / np.float32(half))).astype(np.float32)
    ang = (pos[:, None] * inv[None, :]).astype(np.float32)
    cos = np.cos(ang).astype(np.float32)
    sin = np.sin(ang).astype(np.float32)
    idx = (np.arange(128) % 64) % 32
    cosT = np.ascontiguousarray(cos[:, idx].T)
    sinT = np.ascontiguousarray(sin[:, idx].T)
    nb = n_tok // 64
    nc_ = n_tok // 16 - 1
    ncp = ((nc_ + 127) // 128) * 128
    c0 = np.arange(nc_) * 16
    b0 = np.arange(nb) * 64
    lo = np.maximum(c0[:, None], b0[None, :])
    hi = np.minimum(c0[:, None] + 32, b0[None, :] + 64)
    selmap = np.zeros((ncp, nb), np.float32)
    selmap[:nc_] = np.clip(hi - lo, 0, None).astype(np.float32) / 32.0
    t = np.arange(n_tok)
    cur = t // 64
    n = np.arange(nb)
    forced = (n[None, :] == 0) | (n[None, :] == cur[:, None]) | (n[None, :] == cur[:, None] - 1)
    bq = np.where(forced, np.float32(1e4), np.float32(0.0)).astype(np.float32)
    bq = np.where(n[None, :] <= cur[:, None], bq, np.float32(-1e30)).astype(np.float32)
    return dict(cosT=cosT, sinT=sinT, selmap=selmap, bq=bq)


PHASES = None
W_NAMES = ["norm_mix", "norm_ffn", "norm_mem", "norm_final", "w_mem_kv", "ffn_w_gate", "ffn_w_up", "ffn_w_down",
           "gmlp_w_in", "gmlp_v_norm", "gmlp_w_s", "gmlp_b_s", "gmlp_w_out",
           "nsa_w_in", "nsa_pe_k", "nsa_pe_v", "nsa_ck_w1", "nsa_ck_w2", "nsa_cv_w1", "nsa_cv_w2", "nsa_w_out"]


def build_program(n_tok, shapes, depth=4, layers=None):
    nc = bass.Bass("TRN2", target_bir_lowering=False)

    def inp(name, shape, dt=F32):
        return nc.dram_tensor(name, list(shape), dt, kind="ExternalInput").ap()

    def scr(name, shape, dt):
        return nc.dram_tensor(name, list(shape), dt, kind="Internal").ap()

    x = inp("x", [n_tok, D])
    mem = inp("mem", [256, D])
    w = {k: inp(k, shapes[k]) for k in W_NAMES}
    nb = n_tok // 64
    ncp = (((n_tok // 16 - 1) + 127) // 128) * 128
    cosT = inp("cosT", [128, n_tok])
    sinT = inp("sinT", [128, n_tok])
    selmap = inp("selmap", [ncp, nb])
    bq = inp("bq", [n_tok, nb])
    out = nc.dram_tensor("out", [n_tok, D], F32, kind="ExternalOutput").ap()
    featT_d = scr("featT_d", [16 * 128, n_tok], BF16)
    vtok_d = scr("vtok_d", [n_tok, 512], BF16)
    gates_d = scr("gates_d", [n_tok, 36], F32)
    memoT_d = scr("memoT_d", [256, n_tok], BF16)
    kcT_d = scr("kcT_d", [256, ncp], BF16)
    vc_d = scr("vc_d", [4, ncp, 64], BF16)
    if layers is None:
        layers = list(range(depth))
    with ExitStack() as es:
        S = Sched(nc, es)
        first_gmlp = bool(layers) and layers[0] % 2 == 0
        if not first_gmlp:
            copy_phase(S, x, out, n_tok)
        for li, i in enumerate(layers):
            j = i // 2
            if i % 2 == 0:
                gmlp_phase(S, out, n_tok, mem, w["norm_mix"][i], w["norm_mem"][i], w["w_mem_kv"][i],
                           w["gmlp_w_in"][j], w["gmlp_v_norm"][j], w["gmlp_w_s"][j], w["gmlp_b_s"][j],
                           w["gmlp_w_out"][j], x_src=(x if li == 0 else None))
            else:
                if PHASES is None or "p" in PHASES:
                    nsa_proj_phase(S, out, n_tok, mem, w["norm_mix"][i], w["norm_mem"][i], w["w_mem_kv"][i],
                                   w["nsa_w_in"][j], cosT, sinT, featT_d, vtok_d, gates_d, memoT_d)
                if PHASES is None or "c" in PHASES:
                    nsa_compress_phase(S, n_tok, featT_d, w["nsa_pe_k"][j], w["nsa_pe_v"][j], w["nsa_ck_w1"][j],
                                       w["nsa_ck_w2"][j], w["nsa_cv_w1"][j], w["nsa_cv_w2"][j], kcT_d, vc_d)
                if PHASES is None or "a" in PHASES:
                    nsa_attn_phase(S, out, n_tok, featT_d, vtok_d, gates_d, memoT_d, kcT_d, vc_d, selmap, bq,
                                   w["nsa_w_out"][j])
            last = (li == len(layers) - 1)
            if PHASES is None or "f" in PHASES:
                ffn_phase(S, out, n_tok, w["ffn_w_gate"][i], w["ffn_w_up"][i], w["ffn_w_down"][i], w["norm_ffn"][i],
                          final_gain=(w["norm_final"] if last else None))
        if not layers or not (PHASES is None or "f" in PHASES):
            final_norm_phase(S, out, out, n_tok, w["norm_final"])
        nc._sched_stats = (S.n_ins, S.n_wait)
    return nc


def kernel(**inputs):
    x = np.ascontiguousarray(np.asarray(inputs["x"], dtype=np.float32))
    mem = np.ascontiguousarray(np.asarray(inputs["mem"], dtype=np.float32))
    B, n_tok, _ = x.shape
    wts = {k: np.ascontiguousarray(np.asarray(inputs[k], dtype=np.float32)) for k in W_NAMES}
    shapes = {k: v.shape for k, v in wts.items()}
    consts = host_constants(n_tok)
    nc = build_program(n_tok, shapes)
    in_maps = []
    for b in range(B):
        m = dict(wts)
        m.update(consts)
        m["x"] = x[b]
        m["mem"] = mem[b]
        in_maps.append(m)
    res = run_bass_kernel_spmd(nc, in_maps, core_ids=list(range(B)))
    return np.stack([np.asarray(r["out"], dtype=np.float32) for r in res.results], axis=0)
```

```python
import math
from contextlib import ExitStack

import numpy as np
import concourse.bass as bass
import concourse.mybir as mybir
from concourse.bass_utils import run_bass_kernel_spmd

F32 = mybir.dt.float32
BF16 = mybir.dt.bfloat16
AF = mybir.ActivationFunctionType
ALU = mybir.AluOpType
AX = mybir.AxisListType

D = 1024
DFF = 2816
NFC = DFF // 128
EPS = 1e-6
N_CORES = 8


class Buf:
    __slots__ = ("t", "name", "writers", "readers", "guard", "dsem", "children", "parent")

    def __init__(self, t, name, parent=None):
        self.t = t
        self.name = name
        self.writers = {}
        self.readers = {}
        self.guard = {}
        self.dsem = None
        self.children = {}
        self.parent = parent

    def __getitem__(self, idx):
        return self.t[idx]

    def sub(self, key):
        c = self.children.get(key)
        if c is None:
            c = Buf(self.t, "%s.%s" % (self.name, key), parent=self)
            c.writers = dict(self.writers)
            c.readers = dict(self.readers)
            c.guard = dict(self.guard)
            self.children[key] = c
        return c

    def nodes(self):
        if self.children:
            return [self] + list(self.children.values())
        return [self]

    def sem_owner(self):
        return self.parent if self.parent is not None else self


def _merge(dst, src):
    for k, v in src.items():
        if dst.get(k, 0) < v:
            dst[k] = v


class Sched:
    CE = ("pe", "dve", "act", "pool")

    def __init__(self, nc, es, n_dma_sems=40):
        self.nc = nc
        self.engs = {"pe": nc.tensor, "dve": nc.vector, "act": nc.scalar, "pool": nc.gpsimd, "sp": nc.sync}
        self.sems = {}
        self.cnt = {}
        for k in self.CE:
            self.sems[k] = es.enter_context(nc.semaphore("sem_" + k))
            self.cnt[k] = 0
        self.free_dsems = []
        for i in range(n_dma_sems):
            key = "d%d" % i
            self.sems[key] = es.enter_context(nc.semaphore("sem_" + key))
            self.cnt[key] = 0
            self.free_dsems.append(key)
        self.waited = {k: {} for k in self.engs}
        self.phase_bufs = []
        self.pes = None
        self.n_ins = 0
        self.n_wait = 0
        self._grp = None

    def begin_phase(self):
        self.pes = ExitStack()
        self.pes.__enter__()
        self.phase_bufs = []
        self.phase_id = getattr(self, "phase_id", 0) + 1

    def end_phase(self):
        self.barrier()
        for b in self.phase_bufs:
            if b.dsem is not None:
                self.free_dsems.append(b.dsem)
                b.dsem = None
        self.phase_bufs = []
        self.pes.__exit__(None, None, None)
        self.pes = None

    def sbuf(self, name, shape, dtype):
        t = self.pes.enter_context(self.nc.sbuf_tensor("p%d_%s" % (self.phase_id, name), list(shape), dtype))
        b = Buf(t, name)
        self.phase_bufs.append(b)
        return b

    def psum(self, name, shape, dtype):
        t = self.pes.enter_context(self.nc.psum_tensor("p%d_%s" % (self.phase_id, name), list(shape), dtype))
        b = Buf(t, name)
        self.phase_bufs.append(b)
        return b

    def _emit_waits(self, eng, deps):
        w = self.waited[eng]
        e = self.engs[eng]
        for k, v in deps.items():
            if k == "pe" and eng == "pe":
                continue
            if w.get(k, 0) >= v:
                continue
            e.wait_ge(self.sems[k], v)
            w[k] = v
            self.n_wait += 1

    def _deps(self, reads, writes):
        deps = {}
        for bb in reads:
            for b in bb.nodes():
                _merge(deps, b.writers)
        for bb in writes:
            for b in bb.nodes():
                if b.readers:
                    g = dict(b.readers)
                    _merge(g, b.writers)
                    b.guard = g
                    b.writers = {}
                    b.readers = {}
                _merge(deps, b.guard)
                _merge(deps, b.writers)
        return deps

    def _record(self, reads, writes, key, val):
        for bb in reads:
            for b in bb.nodes():
                if b.readers.get(key, 0) < val:
                    b.readers[key] = val
        for bb in writes:
            for b in bb.nodes():
                if b.writers.get(key, 0) < val:
                    b.writers[key] = val

    def op(self, eng, fn, reads=(), writes=(), signal=True):
        deps = self._deps(reads, writes)
        self._emit_waits(eng, deps)
        ins = fn()
        self.n_ins += 1
        if signal:
            self.cnt[eng] += 1
            ins.then_inc(self.sems[eng], 1)
            val = self.cnt[eng]
        else:
            val = self.cnt[eng] + 1
        self._record(reads, writes, eng, val)
        return ins

    def dma(self, q, out, in_, reads=(), writes=(), sem_buf=None, **kw):
        if sem_buf is None:
            sem_buf = writes[0] if writes else reads[0]
        sem_buf = sem_buf.sem_owner()
        if sem_buf.dsem is None:
            sem_buf.dsem = self.free_dsems.pop()
        key = sem_buf.dsem
        deps = self._deps(reads, writes)
        self._emit_waits(q, deps)
        ins = self.engs[q].dma_start(out=out, in_=in_, **kw)
        self.cnt[key] += 16
        ins.then_inc(self.sems[key], 16)
        self.n_ins += 1
        if self._grp is not None:
            self._grp.append((reads, writes, key))
        else:
            self._record(reads, writes, key, self.cnt[key])
        return ins

    def dma_group_begin(self):
        self._grp = []

    def dma_group_end(self):
        for reads, writes, key in self._grp:
            self._record(reads, writes, key, self.cnt[key])
        self._grp = None

    def barrier(self):
        allv = {k: v for k, v in self.cnt.items() if v > 0}
        for eng in self.engs:
            deps = {k: v for k, v in allv.items() if k != eng or eng == "sp"}
            if eng in self.CE and self.cnt[eng] > 0:
                deps[eng] = self.cnt[eng]
            w = self.waited[eng]
            e = self.engs[eng]
            for k, v in deps.items():
                if w.get(k, 0) >= v:
                    continue
                e.wait_ge(self.sems[k], v)
                w[k] = v


def make_identity(S, name="ident", dtype=BF16):
    nc = S.nc
    ones = S.sbuf(name + "_ones", [128, 128], dtype)
    ident = S.sbuf(name, [128, 128], dtype)
    S.op("pool", lambda: nc.gpsimd.memset(ones[:], 1.0), writes=[ones])
    S.op("pool", lambda: nc.gpsimd.affine_select(
        out=ident[:], in_=ones[:], pattern=[[1, 128]], compare_op=ALU.is_equal, fill=0.0,
        base=0, channel_multiplier=-1), reads=[ones], writes=[ident])
    return ident


_cast_rr = [0]


def cast_copy(S, out_ap, in_ap, reads, writes, engines=("dve", "pool", "act")):
    nc = S.nc
    e = engines[_cast_rr[0] % len(engines)]
    _cast_rr[0] += 1
    if e == "dve":
        S.op("dve", lambda: nc.vector.tensor_copy(out=out_ap, in_=in_ap), reads=reads, writes=writes)
    elif e == "pool":
        S.op("pool", lambda: nc.gpsimd.tensor_copy(out=out_ap, in_=in_ap), reads=reads, writes=writes)
    else:
        S.op("act", lambda: nc.scalar.copy(out=out_ap, in_=in_ap), reads=reads, writes=writes)


def load_gain_bc(S, name, g_ap_1d, n):
    t = S.sbuf(name, [128, n], F32)
    S.dma("sp", t[:], g_ap_1d.partition_broadcast(128), writes=[t])
    return t


def rms_rstd(S, ssq, rstd, n_feat, ncols, c0=0):
    nc = S.nc
    if getattr(S, "_mhalf_phase", None) != S.phase_id:
        S._mhalf = S.sbuf("mhalf", [128, 16], F32)
        S._mhalf_phase = S.phase_id
        S.op("pool", lambda: nc.gpsimd.memset(S._mhalf[:], -0.5), writes=[S._mhalf])
    mh = S._mhalf
    S.op("dve", lambda: nc.vector.tensor_scalar(
        out=rstd[:, c0:ncols], in0=ssq[:, c0:ncols], scalar1=1.0 / n_feat, scalar2=EPS,
        op0=ALU.mult, op1=ALU.add), reads=[ssq], writes=[rstd])
    S.op("pool", lambda: nc.gpsimd.tensor_tensor(
        out=rstd[:, c0:ncols], in0=rstd[:, c0:ncols], in1=mh[:, c0:ncols], op=ALU.pow),
        reads=[rstd, mh], writes=[rstd])


def ffn_phase(S, x_dram, n_tok, wg, wu, wd, gain, T=256, final_gain=None):
    nc = S.nc
    S.begin_phase()
    nsub = T // 128
    ntile = n_tok // T
    ident = make_identity(S)
    g_bc = load_gain_bc(S, "g_bc", gain, D)
    gf_bc = load_gain_bc(S, "gf_bc", final_gain, D) if final_gain is not None else None
    fssq = S.sbuf("fssq", [128, 8], F32)
    frstd = S.sbuf("frstd", [128, 8], F32)

    wg_sb = S.sbuf("wg_sb", [128, 8, DFF], BF16)
    wu_sb = S.sbuf("wu_sb", [128, 8, DFF], BF16)
    wd_sb = S.sbuf("wd_sb", [128, NFC, D], BF16)
    HALF = DFF // 2
    stage = [S.sbuf("stage%d" % i, [128, HALF], F32) for i in range(2)]
    si = 0
    for w_dram, w_sb in ((wg, wg_sb), (wu, wu_sb)):
        for k in range(8):
            for hf in range(2):
                st = stage[si % 2]
                si += 1
                S.dma("sp", st[:], w_dram[k * 128:(k + 1) * 128, hf * HALF:(hf + 1) * HALF], writes=[st])
                cast_copy(S, w_sb[:, k, hf * HALF:(hf + 1) * HALF], st[:], [st], [w_sb])
    wd_v = wd.rearrange("(c p) n -> p c n", p=128)
    for c in range(NFC):
        st = stage[si % 2]
        si += 1
        S.dma("sp", st[:, 0:D], wd_v[:, c, :], writes=[st])
        cast_copy(S, wd_sb[:, c, :], st[:, 0:D], [st], [wd_sb])

    xt = [S.sbuf("xt%d" % i, [128, nsub, D], F32) for i in range(2)]
    hT = [S.sbuf("hT%d" % i, [128, 8, T], BF16) for i in range(2)]
    h = S.sbuf("h", [128, nsub, D], BF16)
    junk = S.sbuf("junk", [128, D], BF16)
    ssq = S.sbuf("ssq", [128, 8], F32)
    rstd = S.sbuf("rstd", [128, 8], F32)
    aT = S.sbuf("aT", [128, NFC, T], BF16)
    sg = [S.sbuf("sg%d" % i, [128, T], F32) for i in range(2)]
    tp = [S.psum("tp%d" % i, [128, 8, 128], BF16) for i in range(2)]
    pg = [S.psum("pg%d" % i, [128, 512], F32) for i in range(2)]
    pu = [S.psum("pu%d" % i, [128, 512], F32) for i in range(2)]
    po = [S.psum("po%d" % i, [128, 512], F32) for i in range(2)]
    x_v = x_dram.rearrange("(t s p) d -> t p s d", p=128, s=nsub)
    cnt = {"tp": 0, "g": 0, "o": 0}

    hs = [h, S.sbuf("h_b", [128, nsub, D], BF16)]

    def stage_a0(i):
        S.dma("sp", xt[i % 2][:], x_v[i], writes=[xt[i % 2]])

    def stage_a1(i):
        X = xt[i % 2]
        hh = hs[i % 2]
        for s in range(nsub):
            S.op("act", lambda s=s: nc.scalar.activation(
                out=junk[:], in_=X[:, s, :], func=AF.Square, accum_out=ssq[:, s:s + 1]),
                reads=[X], writes=[junk, ssq])
        rms_rstd(S, ssq, rstd, D, nsub)
        for s in range(nsub):
            S.op("dve", lambda s=s: nc.vector.scalar_tensor_tensor(
                out=hh[:, s, :], in0=X[:, s, :], scalar=rstd[:, s:s + 1], in1=g_bc[:],
                op0=ALU.mult, op1=ALU.mult), reads=[X, rstd, g_bc], writes=[hh])

    def stage_a2(i):
        HT = hT[i % 2]
        hh = hs[i % 2]
        for s in range(nsub):
            P = tp[cnt["tp"] % 2]
            cnt["tp"] += 1
            for k in range(8):
                S.op("pe", lambda k=k, s=s, P=P: nc.tensor.transpose(
                    out=P[:, k, :], in_=hh[:, s, k * 128:(k + 1) * 128], identity=ident[:]),
                    reads=[hh, ident], writes=[P], signal=(k == 7))
            S.op("act", lambda s=s, P=P: nc.scalar.copy(out=HT[:, :, s * 128:(s + 1) * 128], in_=P[:]),
                 reads=[P], writes=[HT])

    def stage_b(i, hook_start=None, hook_mid=None):
        HT = hT[i % 2]
        for fc in range(NFC):
            if fc == 1 and hook_start is not None:
                hook_start()
            if fc == NFC // 2 and hook_mid is not None:
                hook_mid()
            G = pg[cnt["g"] % 2]
            U = pu[cnt["g"] % 2]
            SG = sg[cnt["g"] % 2]
            cnt["g"] += 1
            for k in range(8):
                S.op("pe", lambda k=k, G=G: nc.tensor.matmul(
                    out=G[:, 0:T], lhsT=wg_sb[:, k, fc * 128:(fc + 1) * 128], rhs=HT[:, k, :],
                    start=(k == 0), stop=(k == 7)), reads=[wg_sb, HT], writes=[G], signal=(k == 7))
            for k in range(8):
                S.op("pe", lambda k=k, U=U: nc.tensor.matmul(
                    out=U[:, 0:T], lhsT=wu_sb[:, k, fc * 128:(fc + 1) * 128], rhs=HT[:, k, :],
                    start=(k == 0), stop=(k == 7)), reads=[wu_sb, HT], writes=[U], signal=(k == 7))
            S.op("act", lambda G=G, SG=SG: nc.scalar.activation(out=SG[:], in_=G[:, 0:T], func=AF.Silu),
                 reads=[G], writes=[SG])
            S.op("dve", lambda U=U, SG=SG: nc.vector.tensor_tensor(
                out=aT[:, fc, :], in0=SG[:], in1=U[:, 0:T], op=ALU.mult), reads=[SG, U], writes=[aT])

    def stage_c(i):
        X = xt[i % 2]
        for s in range(nsub):
            for hf in range(2):
                O = po[cnt["o"] % 2]
                cnt["o"] += 1
                for fc in range(NFC):
                    S.op("pe", lambda fc=fc, O=O: nc.tensor.matmul(
                        out=O[:], lhsT=aT[:, fc, s * 128:(s + 1) * 128], rhs=wd_sb[:, fc, hf * 512:(hf + 1) * 512],
                        start=(fc == 0), stop=(fc == NFC - 1)), reads=[aT, wd_sb], writes=[O],
                        signal=(fc == NFC - 1))
                S.op("dve", lambda O=O: nc.vector.tensor_tensor(
                    out=X[:, s, hf * 512:(hf + 1) * 512], in0=X[:, s, hf * 512:(hf + 1) * 512], in1=O[:],
                    op=ALU.add), reads=[X, O], writes=[X])
        if gf_bc is not None:
            for s in range(nsub):
                S.op("act", lambda s=s: nc.scalar.activation(
                    out=junk[:], in_=X[:, s, :], func=AF.Square, accum_out=fssq[:, s:s + 1]),
                    reads=[X], writes=[junk, fssq])
            rms_rstd(S, fssq, frstd, D, nsub)
            for s in range(nsub):
                S.op("dve", lambda s=s: nc.vector.scalar_tensor_tensor(
                    out=X[:, s, :], in0=X[:, s, :], scalar=frstd[:, s:s + 1], in1=gf_bc[:],
                    op0=ALU.mult, op1=ALU.mult), reads=[X, frstd, gf_bc], writes=[X])
        S.dma("sp", x_v[i], X[:], reads=[X])

    stage_a0(0)
    stage_a1(0)
    stage_a2(0)
    for i in range(ntile):
        if i + 1 < ntile:
            stage_b(i, hook_start=lambda: stage_a0(i + 1), hook_mid=lambda: stage_a1(i + 1))
            stage_a2(i + 1)
        else:
            stage_b(i)
        stage_c(i)
    S.end_phase()


def final_norm_phase(S, x_dram, out_dram, n_tok, gain, T=512):
    nc = S.nc
    S.begin_phase()
    nsub = T // 128
    ntile = n_tok // T
    g_bc = load_gain_bc(S, "g_bc", gain, D)
    xt = [S.sbuf("xt%d" % i, [128, nsub, D], F32) for i in range(2)]
    junk = S.sbuf("junk", [128, D], BF16)
    ssq = S.sbuf("ssq", [128, 8], F32)
    rstd = S.sbuf("rstd", [128, 8], F32)
    x_v = x_dram.rearrange("(t s p) d -> t p s d", p=128, s=nsub)
    o_v = out_dram.rearrange("(t s p) d -> t p s d", p=128, s=nsub)
    for i in range(ntile):
        X = xt[i % 2]
        S.dma("sp", X[:], x_v[i], writes=[X])
        for s in range(nsub):
            S.op("act", lambda s=s: nc.scalar.activation(
                out=junk[:], in_=X[:, s, :], func=AF.Square, accum_out=ssq[:, s:s + 1]),
                reads=[X], writes=[junk, ssq])
        rms_rstd(S, ssq, rstd, D, nsub)
        for s in range(nsub):
            S.op("dve", lambda s=s: nc.vector.scalar_tensor_tensor(
                out=X[:, s, :], in0=X[:, s, :], scalar=rstd[:, s:s + 1], in1=g_bc[:],
                op0=ALU.mult, op1=ALU.mult), reads=[X, rstd, g_bc], writes=[X])
        S.dma("sp", o_v[i], X[:], reads=[X])
    S.end_phase()


def copy_phase(S, src_dram, dst_dram, n_tok, T=512):
    S.begin_phase()
    nsub = T // 128
    xt = [S.sbuf("ct%d" % i, [128, nsub, D], F32) for i in range(2)]
    s_v = src_dram.rearrange("(t s p) d -> t p s d", p=128, s=nsub)
    d_v = dst_dram.rearrange("(t s p) d -> t p s d", p=128, s=nsub)
    for i in range(n_tok // T):
        X = xt[i % 2]
        S.dma("sp", X[:], s_v[i], writes=[X])
        S.dma("sp", d_v[i], X[:], reads=[X])
    S.end_phase()


def load_weight_bf16(S, w_sb, w_dram, stage, col_lo=0, col_hi=None, dst_lo=0, q="sp"):
    nk = w_dram.shape[0] // 128
    if col_hi is None:
        col_hi = w_dram.shape[1]
    n = col_hi - col_lo
    for k in range(nk):
        st = stage[S._stage_i % len(stage)]
        S._stage_i += 1
        S.dma(q, st[:, 0:n], w_dram[k * 128:(k + 1) * 128, col_lo:col_hi], writes=[st])
        cast_copy(S, w_sb[:, k, dst_lo:dst_lo + n], st[:, 0:n], [st], [w_sb])


def mem_kv_prep(S, banks, tpb, ident, mem_b, g_mem, w_kv, stage):
    nc = S.nc
    gm_bc = load_gain_bc(S, "gm_bc", g_mem, D)
    wkv_sb = S.sbuf("wkv_sb", [128, 8, 512], BF16)
    load_weight_bf16(S, wkv_sb, w_kv, stage)
    mx = S.sbuf("mem_x", [128, 2, D], F32)
    mh = S.sbuf("mem_h", [128, 2, D], BF16)
    mjunk = S.sbuf("mem_junk", [128, D], BF16)
    mssq = S.sbuf("mem_ssq", [128, 8], F32)
    mrstd = S.sbuf("mem_rstd", [128, 8], F32)
    memT = S.sbuf("memT", [128, 8, 256], BF16)
    kT = S.sbuf("kT_mem", [128, 2, 256], BF16)
    vm = S.sbuf("v_mem", [128, 2, 256], BF16)
    S.dma("sp", mx[:], mem_b.rearrange("(s p) d -> p s d", p=128), writes=[mx])
    for s in range(2):
        S.op("act", lambda s=s: nc.scalar.activation(
            out=mjunk[:], in_=mx[:, s, :], func=AF.Square, accum_out=mssq[:, s:s + 1]),
            reads=[mx], writes=[mjunk, mssq])
    rms_rstd(S, mssq, mrstd, D, 2)
    for s in range(2):
        S.op("dve", lambda s=s: nc.vector.scalar_tensor_tensor(
            out=mh[:, s, :], in0=mx[:, s, :], scalar=mrstd[:, s:s + 1], in1=gm_bc[:],
            op0=ALU.mult, op1=ALU.mult), reads=[mx, mrstd, gm_bc], writes=[mh])
    for s in range(2):
        for k in range(8):
            S.op("pe", lambda k=k, s=s: nc.tensor.transpose(
                out=tpb[:, k, :], in_=mh[:, s, k * 128:(k + 1) * 128], identity=ident[:]),
                reads=[mh, ident], writes=[tpb], signal=(k == 7))
        S.op("act", lambda s=s: nc.scalar.copy(out=memT[:, :, s * 128:(s + 1) * 128], in_=tpb[:]),
             reads=[tpb], writes=[memT])
    pb = banks[0]
    for c in range(2):
        for k in range(8):
            S.op("pe", lambda k=k, c=c: nc.tensor.matmul(
                out=pb[:, 0:256], lhsT=wkv_sb[:, k, c * 128:(c + 1) * 128], rhs=memT[:, k, :],
                start=(k == 0), stop=(k == 7)), reads=[wkv_sb, memT], writes=[pb], signal=(k == 7))
        S.op("dve", lambda c=c: nc.vector.tensor_copy(out=kT[:, c, :], in_=pb[:, 0:256]), reads=[pb], writes=[kT])
    for mc in range(2):
        for k in range(8):
            S.op("pe", lambda k=k, mc=mc: nc.tensor.matmul(
                out=pb[:, 0:256], lhsT=memT[:, k, mc * 128:(mc + 1) * 128], rhs=wkv_sb[:, k, 256:512],
                start=(k == 0), stop=(k == 7)), reads=[wkv_sb, memT], writes=[pb], signal=(k == 7))
        S.op("dve", lambda mc=mc: nc.vector.tensor_copy(out=vm[:, mc, :], in_=pb[:, 0:256]), reads=[pb], writes=[vm])
    return kT, vm


def norm_transpose(S, X, h, hT, junk, ssq, rstd, g_bc, tpb, ident, nsub):
    nc = S.nc
    for s in range(nsub):
        S.op("act", lambda s=s: nc.scalar.activation(
            out=junk[:], in_=X[:, s, :], func=AF.Square, accum_out=ssq[:, s:s + 1]),
            reads=[X], writes=[junk, ssq])
    rms_rstd(S, ssq, rstd, D, nsub)
    for s in range(nsub):
        S.op("dve", lambda s=s: nc.vector.scalar_tensor_tensor(
            out=h[:, s, :], in0=X[:, s, :], scalar=rstd[:, s:s + 1], in1=g_bc[:],
            op0=ALU.mult, op1=ALU.mult), reads=[X, rstd, g_bc], writes=[h])
    for s in range(nsub):
        for k in range(8):
            S.op("pe", lambda k=k, s=s: nc.tensor.transpose(
                out=tpb[:, k, :], in_=h[:, s, k * 128:(k + 1) * 128], identity=ident[:]),
                reads=[h, ident], writes=[tpb], signal=(k == 7))
        S.op("act", lambda s=s: nc.scalar.copy(out=hT[:, :, s * 128:(s + 1) * 128], in_=tpb[:]),
             reads=[tpb], writes=[hT])


def mem_attention(S, kT, vm, ones_bf, mqT, catT, cat_base, sbanks, pmo, pden, pTs, rden, T):
    nc = S.nc
    for hd in range(4):
        ch = hd // 2
        po = (hd % 2) * 64
        pst = sbanks[hd]
        for mc in range(2):
            S.op("pe", lambda mc=mc, pst=pst, po=po, ch=ch: nc.tensor.matmul(
                out=pst[:, mc * T:(mc + 1) * T], lhsT=kT[po:po + 64, ch, mc * 128:(mc + 1) * 128],
                rhs=mqT[po:po + 64, ch, :], start=True, stop=True),
                reads=[kT, mqT], writes=[pst], signal=(mc == 1))
    for hd in range(4):
        S.op("act", lambda hd=hd: nc.scalar.activation(
            out=pTs[hd][:, 0:2 * T], in_=sbanks[hd][:, 0:2 * T], func=AF.Exp, scale=0.125),
            reads=[sbanks[hd]], writes=[pTs[hd]])
    for hd in range(4):
        ch = hd // 2
        po = (hd % 2) * 64
        pT = pTs[hd]
        for mc in range(2):
            S.op("pe", lambda mc=mc, pT=pT, po=po, ch=ch, hd=hd: nc.tensor.matmul(
                out=pmo[po:po + 64, ch * T:(ch + 1) * T], lhsT=vm[:, mc, hd * 64:(hd + 1) * 64],
                rhs=pT[:, mc * T:(mc + 1) * T], start=(mc == 0), stop=(mc == 1)),
                reads=[vm, pT], writes=[pmo], signal=(mc == 1))
        for mc in range(2):
            S.op("pe", lambda mc=mc, pT=pT, po=po, ch=ch: nc.tensor.matmul(
                out=pden[po:po + 64, ch * T:(ch + 1) * T], lhsT=ones_bf[:, 0:64],
                rhs=pT[:, mc * T:(mc + 1) * T], start=(mc == 0), stop=(mc == 1)),
                reads=[ones_bf, pT], writes=[pden], signal=(mc == 1))
    S.op("dve", lambda: nc.vector.reciprocal(out=rden[:, 0:2 * T], in_=pden[:, 0:2 * T]), reads=[pden], writes=[rden])
    S.op("dve", lambda: nc.vector.tensor_tensor(
        out=catT[:, cat_base:cat_base + 2, :], in0=pmo[:, 0:2 * T].rearrange("p (c t) -> p c t", c=2),
        in1=rden[:, 0:2 * T].rearrange("p (c t) -> p c t", c=2), op=ALU.mult),
        reads=[pmo, rden], writes=[catT])


def out_proj_store(S, X, catT, wout_sb, po_banks, x_dst, nsub, cnt):
    nc = S.nc
    for s in range(nsub):
        for hf in range(2):
            O = po_banks[cnt["o"] % len(po_banks)]
            cnt["o"] += 1
            for fc in range(8):
                S.op("pe", lambda fc=fc, O=O: nc.tensor.matmul(
                    out=O[:, 0:512], lhsT=catT[:, fc, s * 128:(s + 1) * 128],
                    rhs=wout_sb[:, fc, hf * 512:(hf + 1) * 512], start=(fc == 0), stop=(fc == 7)),
                    reads=[catT, wout_sb], writes=[O], signal=(fc == 7))
            S.op("dve", lambda O=O: nc.vector.tensor_tensor(
                out=X[:, s, hf * 512:(hf + 1) * 512], in0=X[:, s, hf * 512:(hf + 1) * 512], in1=O[:, 0:512],
                op=ALU.add), reads=[X, O], writes=[X])
    S.dma("sp", x_dst, X[:], reads=[X])


def gmlp_phase(S, x_dram, n_tok, mem_b, g_mix, g_mem, w_kv, w_in, v_gain, w_s, b_s, w_out, T=256, x_src=None):
    nc = S.nc
    S.begin_phase()
    S._stage_i = 0
    nsub = T // 128
    ntile = n_tok // T
    GW = 768
    ident = make_identity(S)
    ones_bf = S.sbuf("ones_bf", [128, 128], BF16)
    S.op("pool", lambda: nc.gpsimd.memset(ones_bf[:], 1.0), writes=[ones_bf])
    g_bc = load_gain_bc(S, "g_bc", g_mix, D)
    vg_bc = load_gain_bc(S, "vg_bc", v_gain, GW)
    tpb = S.psum("tpb", [128, 8, 128], BF16)
    banks = [S.psum("bank%d" % i, [128, 512], F32) for i in range(7)]
    stage = [S.sbuf("stage%d" % i, [128, 1792], F32) for i in range(2)]

    kT, vm = mem_kv_prep(S, banks, tpb, ident, mem_b, g_mem, w_kv, stage)

    win_sb = S.sbuf("win_sb", [128, 8, 1792], BF16)
    wout_sb = S.sbuf("wout_sb", [128, 8, D], BF16)
    load_weight_bf16(S, win_sb, w_in, stage)
    load_weight_bf16(S, wout_sb, w_out, stage)

    ws_nat = S.sbuf("ws_nat", [128, 12, 128], F32)
    ws_bf = S.sbuf("ws_bf", [128, 12, 128], BF16)
    wsT = S.sbuf("wsT", [128, 12, 128], BF16)
    S.dma("sp", ws_nat[:], w_s.rearrange("g t s -> t g s"), writes=[ws_nat])
    S.op("pool", lambda: nc.gpsimd.affine_select(
        out=ws_bf[:], in_=ws_nat[:], pattern=[[0, 12], [-1, 128]], compare_op=ALU.is_ge, fill=0.0,
        base=0, channel_multiplier=1), reads=[ws_nat], writes=[ws_bf])
    for g8 in range(2):
        ng = 8 if g8 == 0 else 4
        for j in range(ng):
            g = g8 * 8 + j
            S.op("pe", lambda g=g, j=j: nc.tensor.transpose(
                out=tpb[:, j, :], in_=ws_bf[:, g, :], identity=ident[:]),
                reads=[ws_bf, ident], writes=[tpb], signal=(j == ng - 1))
        S.op("act", lambda g8=g8, ng=ng: nc.scalar.copy(out=wsT[:, g8 * 8:g8 * 8 + ng, :], in_=tpb[:, 0:ng, :]),
             reads=[tpb], writes=[wsT])
    bs_f = S.sbuf("bs_f", [33, 12 * 128], F32)
    bs_t = S.sbuf("bs_t", [33, 12 * 128], BF16)
    bs2 = S.sbuf("bs2", [33, 12 * 128], BF16)
    bflat = b_s.rearrange("g t -> (g t)")
    S.dma("sp", bs_f[0:1, :], bflat.partition_broadcast(1), writes=[bs_f])
    S.dma("sp", bs_f[32:33, :], bflat.partition_broadcast(1), writes=[bs_f])
    S.op("dve", lambda: nc.vector.memset(bs2[:], 0.0), writes=[bs2])
    S.op("dve", lambda: nc.vector.tensor_copy(out=bs2[0:1, :], in_=bs_f[0:1, :]), reads=[bs_f], writes=[bs2])
    S.op("dve", lambda: nc.vector.tensor_copy(out=bs_t[32:33, :], in_=bs_f[32:33, :]), reads=[bs_f], writes=[bs_t])
    S.op("dve", lambda: nc.vector.tensor_tensor(out=bs2[32:33, :], in0=bs_f[32:33, :], in1=bs_t[32:33, :],
                                                op=ALU.subtract), reads=[bs_f, bs_t], writes=[bs2])

    xt = [S.sbuf("xt%d" % i, [128, nsub, D], F32) for i in range(2)]
    hT = [S.sbuf("hT%d" % i, [128, 8, T], BF16) for i in range(2)]
    h = S.sbuf("h", [128, nsub, D], BF16)
    junk = S.sbuf("junk", [128, D], BF16)
    ssq = S.sbuf("ssq", [128, 8], F32)
    rstd = S.sbuf("rstd", [128, 8], F32)
    vssq = S.sbuf("vssq", [128, 8], F32)
    vrstd = S.sbuf("vrstd", [128, 8], F32)
    uT = S.sbuf("uT", [128, 6, T], BF16)
    mqT = S.sbuf("mqT", [128, 2, T], BF16)
    vfs = [S.sbuf("vf%d" % i, [128, GW], F32) for i in range(2)]
    vn = [S.sbuf("vn%d" % i, [128, GW], BF16) for i in range(nsub)]
    catT = S.sbuf("catT", [128, 8, T], BF16)
    pTs = [S.sbuf("pT%d" % i, [128, 2 * T], BF16) for i in range(4)]
    rden = S.sbuf("rden", [128, 2 * T], F32)
    x_v = x_dram.rearrange("(t s p) d -> t p s d", p=128, s=nsub)
    x_in = x_v if x_src is None else x_src.rearrange("(t s p) d -> t p s d", p=128, s=nsub)
    cnt = {"o": 0, "b1": 0}

    def stage_a(i):
        X = xt[i % 2]
        S.dma("sp", X[:], x_in[i], writes=[X])
        norm_transpose(S, X, h, hT[i % 2], junk, ssq, rstd, g_bc, tpb, ident, nsub)

    def stage_b(i):
        HT = hT[i % 2]
        for j in range(8):
            col = j * 128 if j < 6 else 1536 + (j - 6) * 128
            P = banks[cnt["b1"] % 2]
            cnt["b1"] += 1
            for k in range(8):
                S.op("pe", lambda k=k, P=P, col=col: nc.tensor.matmul(
                    out=P[:, 0:T], lhsT=win_sb[:, k, col:col + 128], rhs=HT[:, k, :],
                    start=(k == 0), stop=(k == 7)), reads=[win_sb, HT], writes=[P], signal=(k == 7))
            if j < 6:
                S.op("act", lambda P=P, j=j: nc.scalar.activation(
                    out=uT[:, j, :], in_=P[:, 0:T], func=AF.Gelu_apprx_tanh), reads=[P], writes=[uT])
            else:
                S.op("dve", lambda P=P, j=j: nc.vector.tensor_copy(out=mqT[:, j - 6, :], in_=P[:, 0:T]),
                     reads=[P], writes=[mqT])
        for s in range(nsub):
            PA, PB = banks[2 + 2 * (s % 2)], banks[3 + 2 * (s % 2)]
            vf = vfs[s % 2]
            for k in range(8):
                S.op("pe", lambda k=k, PA=PA: nc.tensor.matmul(
                    out=PA[:, 0:512], lhsT=HT[:, k, s * 128:(s + 1) * 128], rhs=win_sb[:, k, 768:1280],
                    start=(k == 0), stop=(k == 7)), reads=[win_sb, HT], writes=[PA], signal=(k == 7))
            for k in range(8):
                S.op("pe", lambda k=k, PB=PB: nc.tensor.matmul(
                    out=PB[:, 0:256], lhsT=HT[:, k, s * 128:(s + 1) * 128], rhs=win_sb[:, k, 1280:1536],
                    start=(k == 0), stop=(k == 7)), reads=[win_sb, HT], writes=[PB], signal=(k == 7))
            S.op("act", lambda PA=PA, vf=vf: nc.scalar.activation(out=vf[:, 0:512], in_=PA[:, 0:512],
                                                                     func=AF.Gelu_apprx_tanh), reads=[PA], writes=[vf])
            S.op("act", lambda PB=PB, vf=vf: nc.scalar.activation(out=vf[:, 512:768], in_=PB[:, 0:256],
                                                                     func=AF.Gelu_apprx_tanh), reads=[PB], writes=[vf])
            S.op("act", lambda s=s, vf=vf: nc.scalar.activation(
                out=junk[:, 0:GW], in_=vf[:], func=AF.Square, accum_out=vssq[:, s:s + 1]),
                reads=[vf], writes=[junk, vssq])
            rms_rstd(S, vssq, vrstd, GW, s + 1, c0=s)
            S.op("dve", lambda s=s, vf=vf: nc.vector.scalar_tensor_tensor(
                out=vn[s][:], in0=vf[:], scalar=vrstd[:, s:s + 1], in1=vg_bc[:],
                op0=ALU.mult, op1=ALU.mult), reads=[vf, vrstd, vg_bc], writes=[vn[s]])
        psp = [banks[4], banks[5], banks[6]]
        for c in range(nsub):
            for g in range(12):
                fc, po = g // 2, (g % 2) * 64
                b, slot = fc // 2, (fc % 2) * T + c * 128
                P = psp[b]
                S.op("pe", lambda g=g, P=P, po=po, slot=slot, c=c: nc.tensor.matmul(
                    out=P[po:po + 64, slot:slot + 128], lhsT=vn[c][:, g * 64:(g + 1) * 64], rhs=wsT[:, g, :],
                    start=True, stop=False), reads=[vn[c], wsT], writes=[P], signal=False)
                S.op("pe", lambda g=g, P=P, po=po, slot=slot: nc.tensor.matmul(
                    out=P[po:po + 64, slot:slot + 128], lhsT=ones_bf[0:33, 0:64], rhs=bs2[0:33, g * 128:(g + 1) * 128],
                    start=False, stop=True), reads=[ones_bf, bs2], writes=[P], signal=True)
        for b in range(3):
            S.op("dve", lambda b=b: nc.vector.tensor_tensor(
                out=catT[:, 2 * b:2 * b + 2, :], in0=psp[b][:, 0:2 * T].rearrange("p (f t) -> p f t", f=2),
                in1=uT[:, 2 * b:2 * b + 2, :], op=ALU.mult), reads=[psp[b], uT], writes=[catT])
        mem_attention(S, kT, vm, ones_bf, mqT, catT, 6, [banks[0], banks[1], banks[4], banks[5]], banks[2], banks[3], pTs, rden, T)

    def stage_c(i):
        out_proj_store(S, xt[i % 2], catT, wout_sb, [banks[0], banks[1]], x_v[i], nsub, cnt)

    stage_a(0)
    for i in range(ntile):
        stage_b(i)
        if i + 1 < ntile:
            stage_a(i + 1)
        stage_c(i)
    S.end_phase()


def nsa_proj_phase(S, x_dram, n_tok, mem_b, g_mix, g_mem, w_kv, w_in, cosT_d, sinT_d,
                   featT_d, vtok_d, gates_d, memoT_d, T=256):
    nc = S.nc
    S.begin_phase()
    S._stage_i = 0
    nsub = T // 128
    ntile = n_tok // T
    ident = make_identity(S)
    ones_bf = S.sbuf("ones_bf", [128, 128], BF16)
    S.op("pool", lambda: nc.gpsimd.memset(ones_bf[:], 1.0), writes=[ones_bf])
    g_bc = load_gain_bc(S, "g_bc", g_mix, D)
    tpb = S.psum("tpb", [128, 8, 128], BF16)
    banks = [S.psum("bank%d" % i, [128, 512], F32) for i in range(7)]
    stage = [S.sbuf("stage%d" % i, [128, 2596], F32) for i in range(2)]
    kT, vm = mem_kv_prep(S, banks, tpb, ident, mem_b, g_mem, w_kv, stage)

    wA = S.sbuf("wA", [128, 8, 2048], BF16)
    wR = S.sbuf("wR", [128, 8, 1536], BF16)
    wB = S.sbuf("wB", [128, 8, 548], BF16)
    mapA = [(0, 0, 768), (768, 768, 256), (1024, 1280, 256), (1280, 1792, 256), (1536, 1024, 256), (1792, 2340, 256)]
    mapB = [(0, 1536, 256), (256, 2048, 256), (512, 2304, 36)]
    for k in range(8):
        st = stage[k % 2]
        S.dma("sp", st[:], w_in[k * 128:(k + 1) * 128, :], writes=[st])
        for (d0, s0, n) in mapA:
            cast_copy(S, wA[:, k, d0:d0 + n], st[:, s0:s0 + n], [st], [wA.sub(k)])
        for (d0, s0, n) in mapA[:4]:
            src = st[:, s0:s0 + n].rearrange("p (h two d) -> p h two d", two=2, d=32)
            dst = wR[:, k, d0:d0 + n].rearrange("p (h two d) -> p h two d", two=2, d=32)
            S.op("dve", lambda src=src, dst=dst: nc.vector.tensor_scalar(
                out=dst[:, :, 0, :], in0=src[:, :, 1, :], scalar1=-1.0, scalar2=None, op0=ALU.mult),
                reads=[st], writes=[wR.sub(k)])
            S.op("pool", lambda src=src, dst=dst: nc.gpsimd.tensor_copy(out=dst[:, :, 1, :], in_=src[:, :, 0, :]),
                 reads=[st], writes=[wR.sub(k)])
        for (d0, s0, n) in mapB:
            cast_copy(S, wB[:, k, d0:d0 + n], st[:, s0:s0 + n], [st], [wB.sub(k)])

    xt = [S.sbuf("xt%d" % i, [128, nsub, D], F32) for i in range(2)]
    hT = [S.sbuf("hT%d" % i, [128, 8, T], BF16) for i in range(2)]
    h = S.sbuf("h", [128, nsub, D], BF16)
    junk = S.sbuf("junk", [128, D], BF16)
    ssq = S.sbuf("ssq", [128, 8], F32)
    rstd = S.sbuf("rstd", [128, 8], F32)
    cs = [S.sbuf("cos%d" % i, [128, T], F32) for i in range(2)]
    sn = [S.sbuf("sin%d" % i, [128, T], F32) for i in range(2)]
    t1s = [S.sbuf("t1_%d" % i, [128, T], F32) for i in range(2)]
    t2s = [S.sbuf("t2_%d" % i, [128, T], F32) for i in range(2)]
    featT = S.sbuf("featT", [128, 14, T], BF16)
    mqT = S.sbuf("mqT", [128, 2, T], BF16)
    memoT = S.sbuf("memoT", [128, 2, T], BF16)
    vtok = S.sbuf("vtok", [128, nsub, 512], BF16)
    gsb = S.sbuf("gsb", [128, nsub, 36], F32)
    pTs = [S.sbuf("pT%d" % i, [128, 2 * T], BF16) for i in range(4)]
    rden = S.sbuf("rden", [128, 2 * T], F32)
    x_v = x_dram.rearrange("(t s p) d -> t p s d", p=128, s=nsub)
    feat_v = featT_d.rearrange("(c p) s -> p c s", p=128)
    memo_v = memoT_d.rearrange("(c p) s -> p c s", p=128)
    vtok_v = vtok_d.rearrange("(t s p) n -> t p s n", p=128, s=nsub)
    gates_v = gates_d.rearrange("(t s p) n -> t p s n", p=128, s=nsub)
    cnt = {"b": 0}

    hs = [h, S.sbuf("h_b", [128, nsub, D], BF16)]

    def stage_a1(i):
        X = xt[i % 2]
        S.dma("sp", X[:], x_v[i], writes=[X])
        S.dma("sp", cs[i % 2][:], cosT_d[:, i * T:(i + 1) * T], writes=[cs[i % 2]])
        S.dma("sp", sn[i % 2][:], sinT_d[:, i * T:(i + 1) * T], writes=[sn[i % 2]])
        hh = hs[i % 2]
        for s in range(nsub):
            S.op("act", lambda s=s: nc.scalar.activation(
                out=junk[:], in_=X[:, s, :], func=AF.Square, accum_out=ssq[:, s:s + 1]),
                reads=[X], writes=[junk, ssq])
        rms_rstd(S, ssq, rstd, D, nsub)
        for s in range(nsub):
            S.op("dve", lambda s=s: nc.vector.scalar_tensor_tensor(
                out=hh[:, s, :], in0=X[:, s, :], scalar=rstd[:, s:s + 1], in1=g_bc[:],
                op0=ALU.mult, op1=ALU.mult), reads=[X, rstd, g_bc], writes=[hh])

    def stage_a2(i):
        hh = hs[i % 2]
        HT = hT[i % 2]
        for s in range(nsub):
            for k in range(8):
                S.op("pe", lambda k=k, s=s: nc.tensor.transpose(
                    out=tpb[:, k, :], in_=hh[:, s, k * 128:(k + 1) * 128], identity=ident[:]),
                    reads=[hh, ident], writes=[tpb], signal=(k == 7))
            S.op("act", lambda s=s: nc.scalar.copy(out=HT[:, :, s * 128:(s + 1) * 128], in_=tpb[:]),
                 reads=[tpb], writes=[HT])

    def proj(P, wsb, col, HT):
        for k in range(8):
            S.op("pe", lambda k=k: nc.tensor.matmul(
                out=P[:, 0:T], lhsT=wsb[:, k, col:col + 128], rhs=HT[:, k, :],
                start=(k == 0), stop=(k == 7)), reads=[wsb, HT], writes=[P], signal=(k == 7))

    def stage_b(i):
        HT = hT[i % 2]
        CS, SN = cs[i % 2], sn[i % 2]
        for c in range(16):
            Pa = banks[(cnt["b"] % 2) * 2]
            Pb = banks[(cnt["b"] % 2) * 2 + 1]
            cnt["b"] += 1
            proj(Pa, wA, c * 128, HT)
            if c < 12:
                t1, t2 = t1s[c % 2], t2s[c % 2]
                proj(Pb, wR, c * 128, HT)
                S.op("dve", lambda Pa=Pa: nc.vector.tensor_tensor(out=t1[:], in0=Pa[:, 0:T], in1=CS[:], op=ALU.mult),
                     reads=[Pa, CS], writes=[t1])
                S.op("dve", lambda Pb=Pb: nc.vector.tensor_tensor(out=t2[:], in0=Pb[:, 0:T], in1=SN[:], op=ALU.mult),
                     reads=[Pb, SN], writes=[t2])
                S.op("pool", lambda c=c: nc.gpsimd.tensor_tensor(out=featT[:, c, :], in0=t1[:], in1=t2[:], op=ALU.add),
                     reads=[t1, t2], writes=[featT.sub(c)])
            elif c < 14:
                S.op("act", lambda Pa=Pa, c=c: nc.scalar.copy(out=featT[:, c, :], in_=Pa[:, 0:T]),
                     reads=[Pa], writes=[featT.sub(c)])
            else:
                S.op("act", lambda Pa=Pa, c=c: nc.scalar.copy(out=mqT[:, c - 14, :], in_=Pa[:, 0:T]),
                     reads=[Pa], writes=[mqT])
        for s in range(nsub):
            PA, PB = banks[4], banks[5]
            for k in range(8):
                S.op("pe", lambda k=k: nc.tensor.matmul(
                    out=PA[:, 0:512], lhsT=HT[:, k, s * 128:(s + 1) * 128], rhs=wB[:, k, 0:512],
                    start=(k == 0), stop=(k == 7)), reads=[wB, HT], writes=[PA], signal=(k == 7))
            for k in range(8):
                S.op("pe", lambda k=k: nc.tensor.matmul(
                    out=PB[:, 0:36], lhsT=HT[:, k, s * 128:(s + 1) * 128], rhs=wB[:, k, 512:548],
                    start=(k == 0), stop=(k == 7)), reads=[wB, HT], writes=[PB], signal=(k == 7))
            S.op("act", lambda s=s: nc.scalar.copy(out=vtok[:, s, :], in_=PA[:, 0:512]), reads=[PA], writes=[vtok])
            S.op("act", lambda s=s: nc.scalar.activation(out=gsb[:, s, :], in_=PB[:, 0:36], func=AF.Exp, scale=-1.0),
                 reads=[PB], writes=[gsb])
            S.op("dve", lambda s=s: nc.vector.tensor_scalar(
                out=gsb[:, s, :], in0=gsb[:, s, :], scalar1=1.0, scalar2=None, op0=ALU.add), reads=[gsb], writes=[gsb])
            S.op("dve", lambda s=s: nc.vector.reciprocal(out=gsb[:, s, :], in_=gsb[:, s, :]), reads=[gsb], writes=[gsb])
        mem_attention(S, kT, vm, ones_bf, mqT, memoT, 0, [banks[0], banks[1], banks[2], banks[3]], banks[4], banks[5], pTs, rden, T)
        S.dma("sp", feat_v[:, 0:14, i * T:(i + 1) * T], featT[:], reads=[featT])
        S.dma("sp", memo_v[:, :, i * T:(i + 1) * T], memoT[:], reads=[memoT])
        S.dma("sp", vtok_v[i], vtok[:], reads=[vtok])
        S.dma("sp", gates_v[i], gsb[:], reads=[gsb])

    stage_a1(0)
    stage_a2(0)
    for i in range(ntile):
        if i + 1 < ntile:
            stage_a1(i + 1)
        stage_b(i)
        if i + 1 < ntile:
            stage_a2(i + 1)
    S.end_phase()


def nsa_compress_phase(S, n_tok, featT_d, pe_k, pe_v, ck_w1, ck_w2, cv_w1, cv_w2, kcT_d, vc_d):
    nc = S.nc
    S.begin_phase()
    NC_ = n_tok // 16 - 1
    ncc = (NC_ + 127) // 128
    NCP = ncc * 128
    banks = [S.psum("bank%d" % i, [128, 512], F32) for i in range(4)]
    kin = S.sbuf("kin", [128, n_tok], BF16)
    w1f = S.sbuf("w1f", [128, 32, 256], F32)
    w1 = S.sbuf("w1", [128, 32, 256], BF16)
    w2f = S.sbuf("w2f", [128, 2, 64], F32)
    w2 = S.sbuf("w2", [128, 2, 64], BF16)
    pef = S.sbuf("pef", [128, 32], F32)
    peb = S.sbuf("peb", [128, 32], BF16)
    bias = S.sbuf("bias", [128, 2], F32)
    hid = S.sbuf("hid", [128, 2, NCP], BF16)
    outk = S.sbuf("outk", [128, NCP], BF16)
    outv = S.sbuf("outv", [128, ncc, 64], BF16)
    S.op("pool", lambda: nc.gpsimd.memset(hid[:], 0.0), writes=[hid])
    S.op("pool", lambda: nc.gpsimd.memset(outk[:], 0.0), writes=[outk])
    for kind in range(2):
        w1_d, w2_d, pe_d = (ck_w1, ck_w2, pe_k) if kind == 0 else (cv_w1, cv_w2, pe_v)
        w1v = w1_d.rearrange("(j d) n -> d j n", d=64)
        for half in range(2):
            S.dma("sp", w1f[half * 64:(half + 1) * 64, :, :], w1v, writes=[w1f])
            S.dma("sp", pef[half * 64:(half + 1) * 64, :], pe_d.rearrange("j d -> d j"), writes=[pef],
                  allow_slow_non_contiguous=True)
        S.dma("sp", w2f[:], w2_d.rearrange("(c p) n -> p c n", p=128), writes=[w2f])
        S.op("dve", lambda: nc.vector.tensor_copy(out=w1[:, 0:16, :], in_=w1f[:, 0:16, :]), reads=[w1f], writes=[w1])
        S.op("pool", lambda: nc.gpsimd.tensor_copy(out=w1[:, 16:32, :], in_=w1f[:, 16:32, :]), reads=[w1f], writes=[w1])
        S.op("dve", lambda: nc.vector.tensor_copy(out=w2[:], in_=w2f[:]), reads=[w2f], writes=[w2])
        S.op("dve", lambda: nc.vector.tensor_copy(out=peb[:], in_=pef[:]), reads=[pef], writes=[peb])
        for hc in range(2):
            for j in range(32):
                S.op("pe", lambda j=j, hc=hc: nc.tensor.matmul(
                    out=banks[3][:, hc:hc + 1], lhsT=w1[0:64, j, hc * 128:(hc + 1) * 128], rhs=peb[0:64, j:j + 1],
                    start=(j == 0), stop=(j == 31)), reads=[w1, peb], writes=[banks[3]], signal=(j == 31))
        S.op("dve", lambda: nc.vector.tensor_copy(out=bias[:], in_=banks[3][:, 0:2]), reads=[banks[3]], writes=[bias])
        for ch in range(2):
            row0 = (768 if kind == 0 else 1536) + ch * 128
            S.dma("sp", kin[:], featT_d[row0:row0 + 128, :], writes=[kin])
            for hh in range(2):
                g = ch * 2 + hh
                po = hh * 64
                for hc in range(2):
                    P = banks[hc]
                    for j in range(32):
                        S.op("pe", lambda j=j, hc=hc, P=P: nc.tensor.matmul(
                            out=P[:, 0:NC_], lhsT=w1[po:po + 64, j, hc * 128:(hc + 1) * 128],
                            rhs=kin[po:po + 64, j:j + 16 * (NC_ - 1) + 1:16],
                            start=(j == 0), stop=(j == 31)), reads=[w1, kin], writes=[P], signal=(j == 31))
                    S.op("act", lambda hc=hc, P=P: nc.scalar.activation(
                        out=hid[:, hc, 0:NC_], in_=P[:, 0:NC_], func=AF.Silu, bias=bias[:, hc:hc + 1]),
                        reads=[P, bias], writes=[hid])
                if kind == 0:
                    P = banks[2]
                    for hc in range(2):
                        S.op("pe", lambda hc=hc: nc.tensor.matmul(
                            out=P[0:64, 0:NCP], lhsT=w2[:, hc, 0:64], rhs=hid[:, hc, :],
                            start=(hc == 0), stop=(hc == 1)), reads=[w2, hid], writes=[P], signal=(hc == 1))
                    S.op("dve", lambda: nc.vector.tensor_copy(out=outk[0:64, 0:NC_], in_=P[0:64, 0:NC_]),
                         reads=[P], writes=[outk])
                    S.dma("sp", kcT_d[g * 64:(g + 1) * 64, :], outk[0:64, :], reads=[outk])
                else:
                    P = banks[2]
                    for cc in range(ncc):
                        for hc in range(2):
                            S.op("pe", lambda hc=hc, cc=cc: nc.tensor.matmul(
                                out=P[:, cc * 64:(cc + 1) * 64], lhsT=hid[:, hc, cc * 128:(cc + 1) * 128],
                                rhs=w2[:, hc, 0:64], start=(hc == 0), stop=(hc == 1)),
                                reads=[w2, hid], writes=[P], signal=(hc == 1))
                    S.op("dve", lambda: nc.vector.tensor_copy(
                        out=outv[:], in_=P[:, 0:ncc * 64].rearrange("p (c d) -> p c d", d=64)), reads=[P], writes=[outv])
                    S.dma("sp", vc_d[g].rearrange("(c p) d -> p c d", p=128), outv[:], reads=[outv])
    S.end_phase()


def nsa_attn_phase(S, x_dram, n_tok, featT_d, vtok_d, gates_d, memoT_d, kcT_d, vc_d, selmap_d, bq_d, w_out):
    nc = S.nc
    S.begin_phase()
    S._stage_i = 0
    NT = n_tok // 128
    NB = n_tok // 64
    NC_ = n_tok // 16 - 1
    ncc_tot = (NC_ + 127) // 128
    NCP = ncc_tot * 128
    nhalf = (NB + 63) // 64
    NBA = min(NB, 64)
    TINY = 1e-30
    identf = make_identity(S, "identf", F32)
    sp = [S.psum("sp%d" % i, [128, 1024], F32) for i in range(2)]
    bS = S.psum("bS", [128, 512], F32)
    bC = S.psum("bC", [128, 512], F32)
    bI = S.psum("bI", [128, 512], F32)
    b7 = S.psum("b7", [128, 512], F32)
    stage = [S.sbuf("stage%d" % i, [128, 512], F32) for i in range(2)]
    wout_sb = S.sbuf("wout_sb", [128, 8, D], BF16)
    load_weight_bf16(S, wout_sb, w_out, stage, col_lo=0, col_hi=512, dst_lo=0)
    load_weight_bf16(S, wout_sb, w_out, stage, col_lo=512, col_hi=1024, dst_lo=512)
    zeros = S.sbuf("zeros", [128, 512], BF16)
    S.op("pool", lambda: nc.gpsimd.memset(zeros[:], 0.0), writes=[zeros])
    ident_bf = make_identity(S, "ident_bf", BF16)
    NEGB = nc.gpsimd.to_reg(-30000.0)
    z3 = zeros[:, 0:384].rearrange("p (r q) -> p r q", r=3)

    def make_bias(dst_ap, dst_buf, pattern, cm, base):
        S.op("pool", lambda: nc.gpsimd.affine_select(
            out=dst_ap, in_=z3, pattern=pattern, compare_op=ALU.is_ge, fill=NEGB, base=base, channel_multiplier=cm),
            reads=[zeros], writes=[dst_buf])

    Mdiag = S.sbuf("Mdiag", [128, 384], BF16)
    Mfar = S.sbuf("Mfar", [128, 384], BF16)
    make_bias(Mdiag[:].rearrange("p (r q) -> p r q", r=3), Mdiag, [[0, 3], [1, 128]], -1, 0)
    make_bias(Mfar[:].rearrange("p (r q) -> p r q", r=3), Mfar, [[0, 3], [-1, 128]], 1, -1)
    Mc = [S.sbuf("Mc%d" % i, [128, 2, 384], BF16) for i in range(2)]

    ksE = S.sbuf("ksE", [128, 4, n_tok], BF16)
    piece = min(2048, n_tok)
    c30k = S.sbuf("c30k", [128, piece], BF16)
    S.op("pool", lambda: nc.gpsimd.memset(c30k[:], 30000.0), writes=[c30k])
    for g in range(4):
        for p0 in range(0, n_tok, piece):
            S.op("pool", lambda g=g, p0=p0: nc.gpsimd.affine_select(
                out=ksE[64:128, g, p0:p0 + piece], in_=c30k[64:128, 0:piece],
                pattern=[[1, piece // 64], [0, 64]], compare_op=ALU.is_equal, fill=0.0,
                base=(p0 // 64) % 64, channel_multiplier=-1), reads=[c30k], writes=[ksE])
    vs_ext = S.sbuf("vs_ext", [128, NT, 4, 65], BF16)
    S.op("pool", lambda: nc.gpsimd.memset(vs_ext[:], 1.0), writes=[vs_ext])
    kw_ring = S.sbuf("kw_ring", [128, 6, 4, 128], BF16)
    vw_ring = S.sbuf("vw_ring", [128, 6, 4, 65], BF16)
    S.op("pool", lambda: nc.gpsimd.memset(vw_ring[:], 1.0), writes=[vw_ring])
    kcT_sb = S.sbuf("kcT_sb", [128, 4, NCP], BF16)
    S.dma("sp", kcT_sb[0:64, :, :], kcT_d.rearrange("(g d) c -> d g c", d=64), writes=[kcT_sb])
    vcR = S.sbuf("vcR", [128, ncc_tot, 4, 65 + NB], BF16)
    S.op("pool", lambda: nc.gpsimd.memset(vcR[:], 1.0), writes=[vcR])
    smf = S.sbuf("smf", [128, ncc_tot, NB], F32)
    S.dma("sp", smf[:], selmap_d.rearrange("(c p) n -> p c n", p=128), writes=[smf])
    for g in range(4):
        S.dma("sp", vcR[:, :, g, 0:64], vc_d[g].rearrange("(c p) d -> p c d", p=128), writes=[vcR])
        S.op("dve", lambda g=g: nc.vector.tensor_copy(out=vcR[:, :, g, 65:65 + NB], in_=smf[:]),
             reads=[smf], writes=[vcR])

    xt = [S.sbuf("xt%d" % i, [128, D], F32) for i in range(2)]
    qA = [S.sbuf("qA%d" % i, [128, 12, 128], BF16) for i in range(2)]
    qB = [S.sbuf("qB%d" % i, [128, 12, 128], BF16) for i in range(2)]
    gts = [S.sbuf("gts%d" % i, [128, 36], F32) for i in range(2)]
    bqs = [S.sbuf("bqs%d" % i, [128, NB], F32) for i in range(2)]
    catT = [S.sbuf("catT%d" % i, [128, 8, 128], BF16) for i in range(2)]
    NPB = 4
    pbuf = [S.sbuf("pbuf%d" % i, [128, 2, 384], BF16) for i in range(NPB)]
    raw = [S.sbuf("raw%d" % i, [128, 12, 65], F32) for i in range(3)]
    rdc = S.sbuf("rdc", [128, 4], F32)
    imp = S.sbuf("imp", [128, NB], F32)
    imp2 = S.sbuf("imp2", [128, NB], F32)
    v8a = S.sbuf("v8a", [128, 8], F32)
    v8b = S.sbuf("v8b", [128, 8], F32)
    selmA_g = [S.sbuf("selmA%d" % i, [128, 128], F32) for i in range(4)]
    selmN_g = [S.sbuf("selmN%d" % i, [128, 128], F32) for i in range(4)]
    for t in selmA_g + selmN_g:
        S.op("dve", lambda t=t: nc.vector.memset(t[:], 0.0), writes=[t])
    den = S.sbuf("den", [128, 12], F32)
    coef = S.sbuf("coef", [128, 12], F32)
    o_acc = S.sbuf("o_acc", [128, 12, 64], F32)
    o_tmp = S.sbuf("o_tmp", [128, 12, 64], F32)
    for t in qA + qB:
        S.op("dve", lambda t=t: nc.vector.memset(t[:], 0.0), writes=[t])

    feat2 = featT_d
    x_v = x_dram.rearrange("(t p) d -> t p d", p=128)
    memo_v = memoT_d.rearrange("(c p) s -> p c s", p=128)
    cnt = {"s": 0, "p": 0}

    def load_tile(i):
        t0 = i * 128
        k = i % 2
        S.dma_group_begin()
        _load_tile(i, t0, k)
        S.dma_group_end()

    def _load_tile(i, t0, k):
        S.dma("sp", xt[k][:], x_v[i], writes=[xt[k]])
        qsrc = feat2[0:768, t0:t0 + 128].rearrange("(h d) s -> d h s", d=64)
        S.dma("sp", qA[k][0:64, :, :], qsrc, writes=[qA[k]], sem_buf=xt[k])
        if nhalf == 2:
            S.dma("sp", qB[k][0:64, :, :], qsrc, writes=[qB[k]], sem_buf=xt[k])
        S.dma("sp", gts[k][:], gates_d[t0:t0 + 128, :], writes=[gts[k]], sem_buf=xt[k])
        S.dma("sp", bqs[k][:], bq_d[t0:t0 + 128, :], writes=[bqs[k]], sem_buf=xt[k])
        S.dma("sp", catT[k][:, 6:8, :], memo_v[:, :, t0:t0 + 128], writes=[catT[k]], sem_buf=xt[k])
        S.dma("sp", ksE[0:64, :, t0:t0 + 128], feat2[1024:1280, t0:t0 + 128].rearrange("(g d) s -> d g s", d=64),
              writes=[ksE.sub(i)], sem_buf=xt[k])
        S.dma("sp", vs_ext[:, i, :, 0:64], vtok_d[t0:t0 + 128, 0:256].rearrange("p (g d) -> p g d", d=64),
              writes=[vs_ext.sub(i)], sem_buf=xt[k])
        sl = i % 6
        S.dma("sp", kw_ring[0:64, sl, :, :], feat2[1280:1536, t0:t0 + 128].rearrange("(g d) s -> d g s", d=64),
              writes=[kw_ring.sub(sl)], sem_buf=xt[k])
        S.dma("sp", vw_ring[:, sl, :, 0:64], vtok_d[t0:t0 + 128, 256:512].rearrange("p (g d) -> p g d", d=64),
              writes=[vw_ring.sub(sl)], sem_buf=xt[k])

    stream = []

    def attend(tiles, outs, pre=None, post=None):
        us = [tiles[u:u + 2] for u in range(0, len(tiles), 2)]
        for ui, tl in enumerate(us):
            stream.append(dict(tiles=tl, outs=outs, first=(ui == 0), last=(ui == len(us) - 1),
                               pre=(pre if ui == 0 else None), post=(post if ui == len(us) - 1 else None)))

    def run_stream():
        n = len(stream)
        spb = {}
        pbs = {}

        def do_score(u):
            U = stream[u]
            if U["pre"] is not None:
                U["pre"]()
            SPb = sp[u % 2]
            nt = len(U["tiles"])
            for t, tl in enumerate(U["tiles"]):
                bias = tl.get("bias", [])
                S.op("pe", lambda t=t, tl=tl, SPb=SPb, bias=bias: nc.tensor.matmul(
                    out=SPb[:, t * 512:t * 512 + 384], lhsT=tl["lhsT"], rhs=tl["rhs"], start=True,
                    stop=(len(bias) == 0)), reads=tl["reads"], writes=[SPb],
                    signal=(t == nt - 1 and len(bias) == 0))
                for bi_, (bap, breads) in enumerate(bias):
                    S.op("pe", lambda t=t, SPb=SPb, bap=bap, bi_=bi_, bias=bias: nc.tensor.matmul(
                        out=SPb[:, t * 512:t * 512 + 384], lhsT=ident_bf[:], rhs=bap, start=False,
                        stop=(bi_ == len(bias) - 1)), reads=[ident_bf] + breads, writes=[SPb],
                        signal=(t == nt - 1 and bi_ == len(bias) - 1))
            spb[u] = SPb

        def do_exp(u):
            U = stream[u]
            SPb = spb.pop(u)
            nt = len(U["tiles"])
            PB = pbuf[u % NPB]
            pbs[u] = PB
            S.op("act", lambda: nc.scalar.activation(
                out=PB[:, 0:nt, :], in_=SPb[:].rearrange("p (t c) -> p t c", t=2)[:, 0:nt, 0:384],
                func=AF.Exp, scale=0.125), reads=[SPb], writes=[PB])

        def do_pv(u):
            U = stream[u]
            PB = pbs.pop(u)
            outs = U["outs"]
            nt = len(U["tiles"])
            if U["first"]:
                for (bank, ncols) in outs:
                    S.op("pe", lambda bank=bank, ncols=ncols: nc.tensor.matmul(
                        out=bank[:, 0:3 * ncols], lhsT=zeros[:, 0:128], rhs=zeros[:, 0:3 * ncols],
                        start=True, stop=False), reads=[zeros], writes=[bank], signal=False)
            for t, tl in enumerate(U["tiles"]):
                fin = U["last"] and t == nt - 1
                for oi, (bank, ncols) in enumerate(outs):
                    rhs_ap, rreads = tl["pv"][oi]
                    for r in range(3):
                        S.op("pe", lambda t=t, r=r, bank=bank, ncols=ncols, rhs_ap=rhs_ap, fin=fin: nc.tensor.matmul(
                            out=bank[:, r * ncols:(r + 1) * ncols], lhsT=PB[:, t, r * 128:(r + 1) * 128], rhs=rhs_ap,
                            start=False, stop=(fin and r == 2)), reads=[PB] + rreads, writes=[bank],
                            signal=(r == 2 and (fin or oi == len(outs) - 1)))

        def do_post(u):
            if u >= 0 and stream[u]["post"] is not None:
                stream[u]["post"]()

        do_score(0)
        if n > 1:
            do_score(1)
        for u in range(n):
            do_exp(u)
            do_post(u - 1)
            if u + 2 < n:
                do_score(u + 2)
            do_pv(u)
        do_post(n - 1)

    MROW = [[0, 3], [1, 128]]

    def comp(i, g):
        k = i % 2
        QA = qA[k]
        ncc_i = min(ncc_tot, (8 * i + 6) // 128 + 1)
        tiles = []
        partial = [cc for cc in range(ncc_i) if 128 * i - 2048 * cc - 31 < 2032]
        assert len(partial) <= 2
        for cc in range(ncc_i):
            bias = []
            if cc in partial:
                bias = [(Mc[k][:, partial.index(cc), :], [Mc[k]])]
            tiles.append(dict(
                lhsT=kcT_sb[0:64, g, cc * 128:(cc + 1) * 128], rhs=QA[0:64, 3 * g:3 * g + 3, :],
                reads=[kcT_sb, QA.sub(g)], bias=bias,
                pv=[(vcR[:, cc, g, 0:65], [vcR]), (vcR[:, cc, g, 65:65 + NB], [vcR])]))

        def pre():
            for pi, cc in enumerate(partial):
                make_bias(Mc[k][:, pi, :].rearrange("p (r q) -> p r q", r=3), Mc[k], [[0, 3], [1, 128]], -16,
                          128 * i - 2048 * cc - 31)
        attend(tiles, [(bC, 65), (bI, NB)], pre=(pre if g == 0 else None), post=lambda: comp_post(i, g))

    def comp_post(i, g):
        k = i % 2
        selmA, selmN = selmA_g[g], selmN_g[g]
        S.op("act", lambda: nc.scalar.copy(
            out=raw[0][:, 3 * g:3 * g + 3, :], in_=bC[:, 0:195].rearrange("p (r e) -> p r e", e=65)),
            reads=[bC], writes=[raw[0].sub(g)])
        S.op("dve", lambda: nc.vector.tensor_scalar(
            out=rdc[:, 0:3], in0=raw[0][:, 3 * g:3 * g + 3, 64], scalar1=TINY, scalar2=None, op0=ALU.max),
            reads=[raw[0].sub(g)], writes=[rdc])
        S.op("dve", lambda: nc.vector.reciprocal(out=rdc[:, 0:3], in_=rdc[:, 0:3]), reads=[rdc], writes=[rdc])
        S.op("dve", lambda: nc.vector.scalar_tensor_tensor(
            out=imp[:], in0=bI[:, 0:NB], scalar=rdc[:, 0:1], in1=bqs[k][:], op0=ALU.mult, op1=ALU.add),
            reads=[bI, rdc, bqs[k]], writes=[imp])
        for r in (1, 2):
            S.op("dve", lambda r=r: nc.vector.scalar_tensor_tensor(
                out=imp[:], in0=bI[:, r * NB:(r + 1) * NB], scalar=rdc[:, r:r + 1], in1=imp[:],
                op0=ALU.mult, op1=ALU.add), reads=[bI, rdc, imp], writes=[imp])
        S.op("dve", lambda: nc.vector.max(out=v8a[:], in_=imp[:]), reads=[imp], writes=[v8a])
        S.op("dve", lambda: nc.vector.match_replace(out=imp2[:], in_to_replace=v8a[:], in_values=imp[:],
                                                    imm_value=-3.0e38), reads=[imp, v8a], writes=[imp2])
        S.op("dve", lambda: nc.vector.max(out=v8b[:], in_=imp2[:]), reads=[imp2], writes=[v8b])
        S.op("dve", lambda: nc.vector.tensor_scalar(
            out=selmA[:, 64:64 + NBA], in0=imp[:, 0:NBA], scalar1=v8b[:, 7:8], scalar2=-1.0,
            op0=ALU.is_ge, op1=ALU.add), reads=[imp, v8b], writes=[selmA])
        if nhalf == 2:
            S.op("dve", lambda: nc.vector.tensor_scalar(
                out=selmN[:, 0:NB], in0=imp[:, 0:NB], scalar1=v8b[:, 7:8], scalar2=-1.0,
                op0=ALU.is_ge, op1=ALU.add), reads=[imp, v8b], writes=[selmN])

    def selT(i, g):
        k = i % 2
        selmA, selmN = selmA_g[g], selmN_g[g]
        S.op("pe", lambda: nc.tensor.transpose(out=bS[:, 256:384], in_=selmA[:], identity=identf[:]),
             reads=[selmA, identf], writes=[bS])
        if nhalf == 2:
            S.op("pe", lambda: nc.tensor.transpose(out=bS[:, 384:512], in_=selmN[:], identity=identf[:]),
                 reads=[selmN, identf], writes=[bS])
        S.op("act", lambda: nc.scalar.copy(
            out=qA[k][64:128, 3 * g:3 * g + 3, :],
            in_=bS[64:128, 256:384].unsqueeze(1).to_broadcast([64, 3, 128])), reads=[bS], writes=[qA[k].sub(g)])
        if nhalf == 2:
            S.op("act", lambda: nc.scalar.copy(
                out=qB[k][64:128, 3 * g:3 * g + 3, :],
                in_=bS[64:128, 384:512].unsqueeze(1).to_broadcast([64, 3, 128])), reads=[bS], writes=[qB[k].sub(g)])

    def sel(i, g):
        k = i % 2
        tiles = []
        for j in range(i + 1):
            Q = qA[k] if j < 32 else qB[k]
            tiles.append(dict(
                lhsT=ksE[:, g, j * 128:(j + 1) * 128], rhs=Q[:, 3 * g:3 * g + 3, :], reads=[ksE.sub(j), Q.sub(g)],
                bias=([(Mdiag[:], [Mdiag])] if j == i else []), pv=[(vs_ext[:, j, g, :], [vs_ext.sub(j)])]))
        def post():
            S.op("act", lambda: nc.scalar.copy(
                out=raw[1][:, 3 * g:3 * g + 3, :], in_=bS[:, 0:195].rearrange("p (r e) -> p r e", e=65)),
                reads=[bS], writes=[raw[1].sub(g)])
            if g == 3:
                finish(i)
                if i == NT - 1:
                    finish_b(i)

        def pre():
            if g == 1 and i > 0:
                finish_b(i - 1)
                if i + 1 < NT:
                    load_tile(i + 1)
            selT(i, g)
        attend(tiles, [(bS, 65)], pre=pre, post=post)

    def win(i, g):
        k = i % 2
        tiles = []
        for j in range(max(0, i - 4), i + 1):
            sl = j % 6
            masks = []
            if j == i:
                masks.append((Mdiag[:], [Mdiag]))
            if j == i - 4:
                masks.append((Mfar[:], [Mfar]))
            tiles.append(dict(
                lhsT=kw_ring[0:64, sl, g, :], rhs=qA[k][0:64, 3 * g:3 * g + 3, :],
                reads=[kw_ring.sub(sl), qA[k].sub(g)], bias=masks, pv=[(vw_ring[:, sl, g, :], [vw_ring.sub(sl)])]))
        attend(tiles, [(b7, 65)], post=lambda: S.op("act", lambda: nc.scalar.copy(
            out=raw[2][:, 3 * g:3 * g + 3, :], in_=b7[:, 0:195].rearrange("p (r e) -> p r e", e=65)),
            reads=[b7], writes=[raw[2].sub(g)]))

    def finish(i):
        k = i % 2
        gv = gts[k][:].rearrange("p (h b) -> p h b", b=3)
        for bi in range(3):
            S.op("dve", lambda bi=bi: nc.vector.tensor_scalar(
                out=den[:], in0=raw[bi][:, :, 64], scalar1=TINY, scalar2=None, op0=ALU.max),
                reads=[raw[bi]], writes=[den])
            S.op("dve", lambda: nc.vector.reciprocal(out=den[:], in_=den[:]), reads=[den], writes=[den])
            S.op("dve", lambda bi=bi: nc.vector.tensor_tensor(out=coef[:], in0=den[:], in1=gv[:, :, bi], op=ALU.mult),
                 reads=[den, gts[k]], writes=[coef])
            dst = o_acc if bi == 0 else o_tmp
            S.op("dve", lambda bi=bi, dst=dst: nc.vector.tensor_tensor(
                out=dst[:], in0=raw[bi][:, :, 0:64], in1=coef[:].unsqueeze(2).to_broadcast([128, 12, 64]),
                op=ALU.mult), reads=[raw[bi], coef], writes=[dst])
            if bi > 0:
                S.op("dve", lambda: nc.vector.tensor_tensor(out=o_acc[:], in0=o_acc[:], in1=o_tmp[:], op=ALU.add),
                     reads=[o_acc, o_tmp], writes=[o_acc])

    def finish_b(i):
        k = i % 2
        oflat = o_acc[:].rearrange("p h d -> p (h d)")
        for half in range(2):
            PB = bS
            for c in range(3):
                fc = half * 3 + c
                S.op("pe", lambda fc=fc, c=c, PB=PB: nc.tensor.transpose(
                    out=PB[:, c * 128:(c + 1) * 128], in_=oflat[:, fc * 128:(fc + 1) * 128], identity=identf[:]),
                    reads=[o_acc, identf], writes=[PB], signal=(c == 2))
            S.op("act", lambda half=half, PB=PB: nc.scalar.copy(
                out=catT[k][:, half * 3:half * 3 + 3, :], in_=PB[:, 0:384].rearrange("p (c q) -> p c q", c=3)),
                reads=[PB], writes=[catT[k]])
        X = xt[k]
        for hf in range(2):
            O = bS
            for fc in range(8):
                S.op("pe", lambda fc=fc, O=O: nc.tensor.matmul(
                    out=O[:, 0:512], lhsT=catT[k][:, fc, :], rhs=wout_sb[:, fc, hf * 512:(hf + 1) * 512],
                    start=(fc == 0), stop=(fc == 7)), reads=[catT[k], wout_sb], writes=[O], signal=(fc == 7))
            S.op("dve", lambda O=O, hf=hf: nc.vector.tensor_tensor(
                out=X[:, hf * 512:(hf + 1) * 512], in0=X[:, hf * 512:(hf + 1) * 512], in1=O[:, 0:512], op=ALU.add),
                reads=[X, O], writes=[X])
        S.dma("sp", x_v[i], X[:], reads=[X])

    load_tile(0)
    if NT > 1:
        load_tile(1)
    for i in range(NT):
        if i < 2:
            comp(i, 0)
            win(i, 0)
            for g in range(4):
                if g + 1 < 4:
                    comp(i, g + 1)
                sel(i, g)
                if g + 1 < 4:
                    win(i, g + 1)
        else:
            comp(i, 0)
            comp(i, 1)
            win(i, 0)
            sel(i, 0)
            comp(i, 2)
            win(i, 1)
            sel(i, 1)
            comp(i, 3)
            win(i, 2)
            sel(i, 2)
            win(i, 3)
            sel(i, 3)
    run_stream()
    S.end_phase()


def host_constants(n_tok):
    half = 32
    pos = np.arange(n_tok, dtype=np.float32)
    inv = (np.float32(10000.0) ** (-np.arange(half, dtype=np.float32) / np.float32(half))).astype(np.float32)
    ang = (pos[:, None] * inv[None, :]).astype(np.float32)
    cos = np.cos(ang).astype(np.float32)
    sin = np.sin(ang).astype(np.float32)
    idx = (np.arange(128) % 64) % 32
    cosT = np.ascontiguousarray(cos[:, idx].T)
    sinT = np.ascontiguousarray(sin[:, idx].T)
    nb = n_tok // 64
    nc_ = n_tok // 16 - 1
    ncp = ((nc_ + 127) // 128) * 128
    c0 = np.arange(nc_) * 16
    b0 = np.arange(nb) * 64
    lo = np.maximum(c0[:, None], b0[None, :])
    hi = np.minimum(c0[:, None] + 32, b0[None, :] + 64)
    selmap = np.zeros((ncp, nb), np.float32)
    selmap[:nc_] = np.clip(hi - lo, 0, None).astype(np.float32) / 32.0
    t = np.arange(n_tok)
    cur = t // 64
    n = np.arange(nb)
    forced = (n[None, :] == 0) | (n[None, :] == cur[:, None]) | (n[None, :] == cur[:, None] - 1)
    bq = np.where(forced, np.float32(1e4), np.float32(0.0)).astype(np.float32)
    bq = np.where(n[None, :] <= cur[:, None], bq, np.float32(-1e30)).astype(np.float32)
    return dict(cosT=cosT, sinT=sinT, selmap=selmap, bq=bq)


PHASES = None
W_NAMES = ["norm_mix", "norm_ffn", "norm_mem", "norm_final", "w_mem_kv", "ffn_w_gate", "ffn_w_up", "ffn_w_down",
           "gmlp_w_in", "gmlp_v_norm", "gmlp_w_s", "gmlp_b_s", "gmlp_w_out",
           "nsa_w_in", "nsa_pe_k", "nsa_pe_v", "nsa_ck_w1", "nsa_ck_w2", "nsa_cv_w1", "nsa_cv_w2", "nsa_w_out"]


def build_program(n_tok, shapes, depth=4, layers=None):
    nc = bass.Bass("TRN2", target_bir_lowering=False)

    def inp(name, shape, dt=F32):
        return nc.dram_tensor(name, list(shape), dt, kind="ExternalInput").ap()

    def scr(name, shape, dt):
        return nc.dram_tensor(name, list(shape), dt, kind="Internal").ap()

    x = inp("x", [n_tok, D])
    mem = inp("mem", [256, D])
    w = {k: inp(k, shapes[k]) for k in W_NAMES}
    nb = n_tok // 64
    ncp = (((n_tok // 16 - 1) + 127) // 128) * 128
    cosT = inp("cosT", [128, n_tok])
    sinT = inp("sinT", [128, n_tok])
    selmap = inp("selmap", [ncp, nb])
    bq = inp("bq", [n_tok, nb])
    out = nc.dram_tensor("out", [n_tok, D], F32, kind="ExternalOutput").ap()
    featT_d = scr("featT_d", [16 * 128, n_tok], BF16)
    vtok_d = scr("vtok_d", [n_tok, 512], BF16)
    gates_d = scr("gates_d", [n_tok, 36], F32)
    memoT_d = scr("memoT_d", [256, n_tok], BF16)
    kcT_d = scr("kcT_d", [256, ncp], BF16)
    vc_d = scr("vc_d", [4, ncp, 64], BF16)
    if layers is None:
        layers = list(range(depth))
    with ExitStack() as es:
        S = Sched(nc, es)
        first_gmlp = bool(layers) and layers[0] % 2 == 0
        if not first_gmlp:
            copy_phase(S, x, out, n_tok)
        for li, i in enumerate(layers):
            j = i // 2
            if i % 2 == 0:
                gmlp_phase(S, out, n_tok, mem, w["norm_mix"][i], w["norm_mem"][i], w["w_mem_kv"][i],
                           w["gmlp_w_in"][j], w["gmlp_v_norm"][j], w["gmlp_w_s"][j], w["gmlp_b_s"][j],
                           w["gmlp_w_out"][j], x_src=(x if li == 0 else None))
            else:
                if PHASES is None or "p" in PHASES:
                    nsa_proj_phase(S, out, n_tok, mem, w["norm_mix"][i], w["norm_mem"][i], w["w_mem_kv"][i],
                                   w["nsa_w_in"][j], cosT, sinT, featT_d, vtok_d, gates_d, memoT_d)
                if PHASES is None or "c" in PHASES:
                    nsa_compress_phase(S, n_tok, featT_d, w["nsa_pe_k"][j], w["nsa_pe_v"][j], w["nsa_ck_w1"][j],
                                       w["nsa_ck_w2"][j], w["nsa_cv_w1"][j], w["nsa_cv_w2"][j], kcT_d, vc_d)
                if PHASES is None or "a" in PHASES:
                    nsa_attn_phase(S, out, n_tok, featT_d, vtok_d, gates_d, memoT_d, kcT_d, vc_d, selmap, bq,
                                   w["nsa_w_out"][j])
            last = (li == len(layers) - 1)
            if PHASES is None or "f" in PHASES:
                ffn_phase(S, out, n_tok, w["ffn_w_gate"][i], w["ffn_w_up"][i], w["ffn_w_down"][i], w["norm_ffn"][i],
                          final_gain=(w["norm_final"] if last else None))
        if not layers or not (PHASES is None or "f" in PHASES):
            final_norm_phase(S, out, out, n_tok, w["norm_final"])
        nc._sched_stats = (S.n_ins, S.n_wait)
    return nc


def kernel(**inputs):
    x = np.ascontiguousarray(np.asarray(inputs["x"], dtype=np.float32))
    mem = np.ascontiguousarray(np.asarray(inputs["mem"], dtype=np.float32))
    B, n_tok, _ = x.shape
    wts = {k: np.ascontiguousarray(np.asarray(inputs[k], dtype=np.float32)) for k in W_NAMES}
    shapes = {k: v.shape for k, v in wts.items()}
    consts = host_constants(n_tok)
    nc = build_program(n_tok, shapes)
    in_maps = []
    for b in range(B):
        m = dict(wts)
        m.update(consts)
        m["x"] = x[b]
        m["mem"] = mem[b]
        in_maps.append(m)
    res = run_bass_kernel_spmd(nc, in_maps, core_ids=list(range(B)))
    return np.stack([np.asarray(r["out"], dtype=np.float32) for r in res.results], axis=0)
```
